# Optimizing a Trainium2 kernel written in Bass

```python
import math
import jax, jax.numpy as jnp
from jax import lax
import numpy as np

D_MODEL = 1024
BATCH = 2
SEQ = 8192
DEPTH = 1

HEAD_DIM = 64
SB_HEADS = D_MODEL // (2 * HEAD_DIM)
SWA_HEADS = D_MODEL // (2 * HEAD_DIM)
SWA_KV_HEADS = max(1, SWA_HEADS // 4)
SB_WIDTH = SB_HEADS * HEAD_DIM
SWA_WIDTH = SWA_HEADS * HEAD_DIM
SWA_KV_WIDTH = SWA_KV_HEADS * HEAD_DIM
MIX_WIDTH = SB_WIDTH + SWA_WIDTH
IN_COLS = 3 * SB_WIDTH + SWA_WIDTH + 2 * SWA_KV_WIDTH
BLOCK = 128
WINDOW = 128
REL_BUCKETS = 32
REL_MAX_DIST = 128
D_FF = -((-8 * D_MODEL) // (3 * 256)) * 256
ALPHA = (2 * DEPTH) ** 0.25
BETA_INIT = (8 * DEPTH) ** -0.25
LN_EPS = 1e-5
RMS_EPS = 1e-6

kernel_name = "stickbreak_swa_sink_hybrid_deepnorm"


def layer_norm(x, g, b):
    xf = x.astype(jnp.float32)
    mu = jnp.mean(xf, axis=-1, keepdims=True)
    var = jnp.mean(jnp.square(xf - mu), axis=-1, keepdims=True)
    return ((xf - mu) * lax.rsqrt(var + LN_EPS)).astype(x.dtype) * g + b


def rms_norm(x, g):
    xf = x.astype(jnp.float32)
    y = xf * lax.rsqrt(jnp.mean(jnp.square(xf), axis=-1, keepdims=True) + RMS_EPS)
    return y.astype(x.dtype) * g


def t5_causal_bucket(distance):
    exact = REL_BUCKETS // 2
    d = jnp.maximum(distance, 0)
    d_f = jnp.maximum(d, 1).astype(jnp.float32)
    large = exact + (jnp.log(d_f / exact) / math.log(REL_MAX_DIST / exact)
                     * (REL_BUCKETS - exact)).astype(jnp.int32)
    large = jnp.minimum(large, REL_BUCKETS - 1)
    return jnp.where(d < exact, d, large)


def stick_breaking_attention(q, k, v):
    B, S, H, Dh = q.shape
    nb = S // BLOCK
    scale = Dh ** -0.5
    qb = q.reshape(B, nb, BLOCK, H, Dh).transpose(1, 0, 3, 2, 4)
    key_pos = jnp.arange(S)

    def one_block(args):
        q_blk, i = args
        z = jnp.einsum('bhqd,bkhd->bhqk', q_blk, k).astype(jnp.float32) * scale
        q_pos = i * BLOCK + jnp.arange(BLOCK)
        causal = key_pos[None, :] < q_pos[:, None]
        log_beta = jax.nn.log_sigmoid(z)
        log_1m_beta = jnp.where(causal, jax.nn.log_sigmoid(-z), 0.0)
        suffix = lax.cumsum(log_1m_beta, axis=3, reverse=True) - log_1m_beta
        attn = jnp.where(causal, jnp.exp(log_beta + suffix), 0.0)
        return jnp.einsum('bhqk,bkhd->bqhd', attn.astype(v.dtype), v)

    out = lax.map(one_block, (qb, jnp.arange(nb)))
    return out.transpose(1, 0, 2, 3, 4).reshape(B, S, H * Dh).astype(v.dtype)


def sliding_window_attention(q, k, v, sinks, rel_bias):
    B, S, H, Dh = q.shape
    KVH = k.shape[2]
    G = H // KVH
    nb = S // BLOCK
    scale = Dh ** -0.5
    qb = q.reshape(B, nb, BLOCK, KVH, G, Dh)

    def band(t):
        tb = t.reshape(B, nb, BLOCK, KVH, Dh)
        prev = jnp.pad(tb[:, :-1], ((0, 0), (1, 0), (0, 0), (0, 0), (0, 0)))
        return jnp.concatenate([prev, tb], axis=2)

    kb, vb = band(k), band(v)
    logits = jnp.einsum('bnqhgd,bnchd->bnhgqc', qb, kb).astype(jnp.float32) * scale

    qi = jnp.arange(BLOCK)[:, None]
    cj = jnp.arange(2 * BLOCK)[None, :]
    dist = qi + BLOCK - cj
    key_abs = jnp.arange(nb)[:, None, None] * BLOCK - BLOCK + cj[None]
    valid = (dist >= 0)[None] & (dist < WINDOW)[None] & (key_abs >= 0)
    bias = rel_bias.astype(jnp.float32)[t5_causal_bucket(dist)]
    bias = bias.transpose(2, 0, 1).reshape(KVH, G, BLOCK, 2 * BLOCK)
    logits = jnp.where(valid[None, :, None, None], logits + bias, -jnp.inf)

    sink = sinks.astype(jnp.float32).reshape(1, 1, KVH, G, 1, 1)
    m = jnp.maximum(jnp.max(logits, axis=-1, keepdims=True), sink)
    p = jnp.exp(logits - m)
    denom = jnp.sum(p, axis=-1, keepdims=True) + jnp.exp(sink - m)
    o = jnp.einsum('bnhgqc,bnchd->bnqhgd', (p / denom).astype(v.dtype), vb)
    return o.reshape(B, S, H * Dh).astype(v.dtype)


def hybrid_mixer(h, w_in, sb_norm_g, swa_norm_g, sinks, rel_bias, w_out):
    B, S, _ = h.shape
    proj = h @ w_in
    o1 = SB_WIDTH
    o2 = o1 + SB_WIDTH
    o3 = o2 + SB_WIDTH
    o4 = o3 + SWA_WIDTH
    o5 = o4 + SWA_KV_WIDTH
    q_sb, k_sb, v_sb, q_sw, k_sw, v_sw = jnp.split(proj, [o1, o2, o3, o4, o5], axis=-1)
    sb_out = stick_breaking_attention(
        q_sb.reshape(B, S, SB_HEADS, HEAD_DIM),
        k_sb.reshape(B, S, SB_HEADS, HEAD_DIM),
        v_sb.reshape(B, S, SB_HEADS, HEAD_DIM))
    swa_out = sliding_window_attention(
        q_sw.reshape(B, S, SWA_HEADS, HEAD_DIM),
        k_sw.reshape(B, S, SWA_KV_HEADS, HEAD_DIM),
        v_sw.reshape(B, S, SWA_KV_HEADS, HEAD_DIM),
        sinks, rel_bias)
    merged = jnp.concatenate([rms_norm(sb_out, sb_norm_g),
                              rms_norm(swa_out, swa_norm_g)], axis=-1)
    return merged @ w_out


def swiglu_ffn(h, w_gate_up, w_down):
    gate, up = jnp.split(h @ w_gate_up, 2, axis=-1)
    return (jax.nn.silu(gate) * up) @ w_down


def setup_inputs(seed: int = 0) -> dict:
    key = jax.random.key(seed)
    ks = jax.random.split(key, 16)
    f32 = jnp.float32
    n = lambda k, shape, s: jax.random.normal(k, shape, f32) * s
    return {
        "x": n(ks[0], (BATCH, SEQ, D_MODEL), 1.0),
        "ln_in_g": 1.0 + n(ks[1], (D_MODEL,), 0.02),
        "ln_in_b": n(ks[2], (D_MODEL,), 0.02),
        "w_in": n(ks[3], (DEPTH, D_MODEL, IN_COLS), D_MODEL ** -0.5),
        "sb_norm_g": 1.0 + n(ks[4], (DEPTH, SB_WIDTH), 0.02),
        "swa_norm_g": 1.0 + n(ks[5], (DEPTH, SWA_WIDTH), 0.02),
        "sinks": n(ks[6], (DEPTH, SWA_HEADS), 0.5),
        "rel_bias": n(ks[7], (REL_BUCKETS, SWA_HEADS), 0.5),
        "w_out": n(ks[8], (DEPTH, MIX_WIDTH, D_MODEL), MIX_WIDTH ** -0.5 * BETA_INIT),
        "ln1_g": 1.0 + n(ks[9], (DEPTH, D_MODEL), 0.02),
        "ln1_b": n(ks[10], (DEPTH, D_MODEL), 0.02),
        "w_gate_up": n(ks[11], (DEPTH, D_MODEL, 2 * D_FF), D_MODEL ** -0.5),
        "w_down": n(ks[12], (DEPTH, D_FF, D_MODEL), D_FF ** -0.5 * BETA_INIT),
        "ln2_g": 1.0 + n(ks[13], (DEPTH, D_MODEL), 0.02),
        "ln2_b": n(ks[14], (DEPTH, D_MODEL), 0.02),
    }


def reference(x, ln_in_g, ln_in_b, w_in, sb_norm_g, swa_norm_g, sinks, rel_bias,
              w_out, ln1_g, ln1_b, w_gate_up, w_down, ln2_g, ln2_b):
    h = layer_norm(x, ln_in_g, ln_in_b)
    for l in range(DEPTH):
        mix = hybrid_mixer(h, w_in[l], sb_norm_g[l], swa_norm_g[l], sinks[l], rel_bias, w_out[l])
        h = layer_norm(ALPHA * h + mix, ln1_g[l], ln1_b[l])
        ffn = swiglu_ffn(h, w_gate_up[l], w_down[l])
        h = layer_norm(ALPHA * h + ffn, ln2_g[l], ln2_b[l])
    return h
```

```python
from contextlib import ExitStack
import numpy as np
import ml_dtypes
import concourse.bass as bass
import concourse.mybir as mybir
from concourse.bass_utils import run_bass_kernel_spmd

F32 = mybir.dt.float32
BF16 = mybir.dt.bfloat16
AF = mybir.ActivationFunctionType
ALU = mybir.AluOpType
AX = mybir.AxisListType

PE, ACT, DVE, POOL, SP = "tensor", "scalar", "vector", "gpsimd", "sync"
ENGS = (PE, ACT, DVE, POOL, SP)

D = 1024
S = 8192
NB = 64
DFF = 2816
NSLOT = 16
ALPHA = 2.0 ** 0.25
LN_EPS = 1e-5
RMS_EPS = 1e-6
NEG = -30000.0


class Buf:
    __slots__ = ("name", "last_w", "readers", "excl")

    def __init__(self, name):
        self.name = name
        self.last_w = None
        self.readers = []
        self.excl = False


class Prog:
    def __init__(self, nc, stack, same_engine_sync=True):
        self.nc = nc
        self.stack = stack
        self.same = same_engine_sync
        self.streams = {e: [] for e in ENGS}
        self.sem = {}
        self.cnt = {}
        self.semobj = {}
        for e in (PE, ACT, DVE, POOL):
            self.sem[e] = self._newsem("c_" + e)
            self.cnt[e] = 0
        self.waited = {e: {} for e in ENGS}
        self.dma_sems = {}
        self.dma_cnt = {}
        self.nbuf = 0
        self.nwaits = 0
        self.limit = None
        self.nops = 0

    def _newsem(self, name):
        s = self.stack.enter_context(self.nc.semaphore(name))
        key = len(self.semobj)
        self.semobj[key] = s
        return key

    def buf(self, name=None):
        self.nbuf += 1
        return Buf(name or f"b{self.nbuf}")

    def bufs(self, n, name="b"):
        return [self.buf(f"{name}{i}") for i in range(n)]

    def _wait(self, eng, s, v):
        if self.waited[eng].get(s, 0) < v:
            self.waited[eng][s] = v
            so = self.semobj[s]
            self.nwaits += 1
            self.streams[eng].append(lambda h, so=so, v=v: h.wait_ge(so, v))

    def _deps(self, eng, reads, writes):
        deps = {}

        def add(tok):
            if tok is None:
                return
            s, v = tok
            if deps.get(s, 0) < v:
                deps[s] = v
        for b in reads:
            add(b.last_w)
        for b in writes:
            add(b.last_w)
            for t in b.readers:
                add(t)
        own = self.sem.get(eng)
        for s, v in deps.items():
            if s == own and (eng == PE or not self.same):
                continue
            self._wait(eng, s, v)

    def _commit(self, tok, reads, writes):
        for b in reads:
            b.readers.append(tok)
        for b in writes:
            b.last_w = tok
            b.readers = []

    def op(self, eng, name, *args, reads=(), writes=(), **kw):
        ex = [b for b in reads if b.excl]
        if ex:
            reads = [b for b in reads if not b.excl]
            writes = list(writes) + ex
        self.nops += 1
        if self.limit is not None and self.nops > self.limit:
            return None
        self._deps(eng, reads, writes)
        self.cnt[eng] += 1
        tok = (self.sem[eng], self.cnt[eng])
        so = self.semobj[self.sem[eng]]
        self.streams[eng].append(
            lambda h, name=name, args=args, kw=kw, so=so: getattr(h, name)(*args, **kw).then_inc(so, 1))
        self._commit(tok, reads, writes)
        return tok

    def group_final(self, semname, bufs):
        tok = (self.dma_sems[semname], self.dma_cnt[semname])
        for b in bufs:
            b.last_w = tok

    def dma(self, q, out, in_, semname, reads=(), writes=()):
        if semname not in self.dma_sems:
            self.dma_sems[semname] = self._newsem("d_" + semname)
            self.dma_cnt[semname] = 0
        s = self.dma_sems[semname]
        if self.limit is not None and self.nops > self.limit:
            return None
        self._deps(q, reads, writes)
        self.dma_cnt[semname] += 16
        tok = (s, self.dma_cnt[semname])
        so = self.semobj[s]
        self.streams[q].append(
            lambda h, out=out, in_=in_, so=so: h.dma_start(out=out, in_=in_).then_inc(so, 16))
        self._commit(tok, reads, writes)
        return tok

    def barrier(self):
        toks = [(self.sem[e], self.cnt[e]) for e in (PE, ACT, DVE, POOL) if self.cnt[e] > 0]
        toks += [(self.dma_sems[n], self.dma_cnt[n]) for n in self.dma_sems if self.dma_cnt[n] > 0]
        for e in ENGS:
            for (s, v) in toks:
                self._wait(e, s, v)

    def emit(self):
        with self.nc.Block() as block:
            def mk(e):
                def body(h):
                    for f in self.streams[e]:
                        f(h)
                return body
            block.tensor(mk(PE))
            block.scalar(mk(ACT))
            block.vector(mk(DVE))
            block.gpsimd(mk(POOL))
            block.sync(mk(SP))


def build(debug=False, same_engine_sync=True, phases="ABCDE", limit=None):
    nc = bass.Bass("TRN2", target_bir_lowering=False)

    def din(name, shape, dt=F32):
        return nc.dram_tensor(name, list(shape), dt, kind="ExternalInput").ap()

    xall = din("xall", [S, D])
    xown = din("xown", [NSLOT * 128, D])
    xprev = din("xprev", [NSLOT * 128, D])
    wA_d = din("wA", [D, 1280])
    wB_d = din("wB", [D, 1024])
    wout_d = din("wout", [D, D])
    wgu_d = din("wgu", [D, 2 * DFF])
    wdn_d = din("wdn", [DFF, D])
    lnin_g = din("lnin_g", [1, D]); lnin_b = din("lnin_b", [1, D])
    ln1_g = din("ln1_g", [1, D]); ln1_b = din("ln1_b", [1, D])
    ln2_g = din("ln2_g", [1, D]); ln2_b = din("ln2_b", [1, D])
    gsb_d = din("gsb", [1, 512]); gsw_d = din("gsw", [1, 512])
    sinks_d = din("sinks", [1, 8])
    swbias_d = din("swbias", [128, 8 * 256])
    swmask_d = din("swmask", [128, 2 * 256])
    sbmask_d = din("sbmask", [128, 8 * 512], BF16)
    cmat_d = din("cmat", [128, 3 * 128], BF16)
    y_d = nc.dram_tensor("y", [NSLOT * 128, D], F32, kind="ExternalOutput").ap()
    h1_d = nc.dram_tensor("h1scr", [NSLOT * 128, D], F32, kind="ExternalOutput").ap()
    dbg = {}
    if debug:
        dbg["k"] = nc.dram_tensor("dbg_k", [128, 4 * S], BF16, kind="ExternalOutput").ap()
        dbg["v"] = nc.dram_tensor("dbg_v", [128, NB * 512], BF16, kind="ExternalOutput").ap()
        dbg["qt"] = nc.dram_tensor("dbg_qt", [128, NSLOT * 512], BF16, kind="ExternalOutput").ap()
        dbg["msw"] = nc.dram_tensor("dbg_msw", [128, NSLOT * 512], BF16, kind="ExternalOutput").ap()
        dbg["msb"] = nc.dram_tensor("dbg_msb", [128, NSLOT * 512], BF16, kind="ExternalOutput").ap()
        dbg["h1"] = nc.dram_tensor("dbg_h1", [NSLOT * 128, D], F32, kind="ExternalOutput").ap()

    with ExitStack() as st:
        P = Prog(nc, st, same_engine_sync)
        P.limit = limit
        sbt = lambda name, shape, dt: st.enter_context(nc.sbuf_tensor(name, list(shape), dt))

        ARW = 33792
        arena = sbt("arena", [128, ARW], F32)
        GENW = 9728
        gen = sbt("gen", [128, GENW], F32)
        persist = sbt("persist", [128, 8192], F32)
        cmat = sbt("cmat_sb", [128, 3, 128], BF16)
        cst = sbt("cst", [128, 4], F32)
        sinkb = sbt("sinkb", [128, 8], F32)
        NST = 4
        st6 = sbt("st6", [128, NST, 12], F32)
        mvt = sbt("mvt", [128, NST, 2], F32)
        rsn = sbt("rsn", [128, NST, 2], F32)
        psum = st.enter_context(nc.psum_tensor("ps", [128, 8, 512], F32))

        ident = cmat[:, 0, :]
        trineg = cmat[:, 1, :]
        onesneg = cmat[:, 2, :]

        class Carver:
            def __init__(self, t, size):
                self.t, self.size, self.off = t, size, 0

            def reset(self):
                self.off = 0

            def f32(self, words):
                a = self.t[:, self.off:self.off + words]
                self.off += words
                assert self.off <= self.size, (self.off, self.size)
                return a

            def bf16(self, elems):
                assert elems % 2 == 0
                return self.f32(elems // 2).bitcast(BF16)

        CA = Carver(arena, ARW)
        CG = Carver(gen, GENW)
        CP = Carver(persist, 8192)

        QM = CP.bf16(NSLOT * 512).rearrange("p (n c q) -> p n c q", n=NSLOT, c=4)
        MSW = CP.bf16(NSLOT * 512).rearrange("p (n c q) -> p n c q", n=NSLOT, c=4)
        bQM = P.bufs(NSLOT, "qm")
        bMSW = P.bufs(NSLOT, "msw")

        pbank = [psum[:, i, :] for i in range(8)]
        bps = P.bufs(8, "ps")
        for b_ in bps:
            b_.excl = True

        def pbf(i, nbanks=1):
            return psum[:, i:i + nbanks, :].rearrange("p b n -> p (b n)").bitcast(BF16)

        bcmat = P.buf("cmat"); bcst = P.buf("cst"); bsink = P.buf("sink")
        bst = P.bufs(NST, "st"); bmv = P.bufs(NST, "mv"); brs = P.bufs(NST, "rs")
        stat_i = [0]

        P.dma(SP, cmat[:].rearrange("p a b -> p (a b)"), cmat_d, "const0", writes=[bcmat])
        P.dma(SP, sinkb[:], sinks_d.partition_broadcast(128), "const1", writes=[bsink])
        P.op(DVE, "memset", cst[:, 0:1], -0.5, writes=[bcst])
        P.op(DVE, "memset", cst[:, 1:2], 1.0, writes=[bcst])
        P.op(DVE, "memset", cst[:, 2:4], 0.0, writes=[bcst])

        def rstd_from(var_ap, var_buf, eps, scale, out_ap, out_buf):
            P.op(DVE, "tensor_scalar", out_ap, var_ap, scale, eps, ALU.mult, ALU.add,
                 reads=[var_buf], writes=[out_buf])
            P.op(POOL, "tensor_tensor", out_ap, out_ap, cst[:, 0:1], ALU.pow,
                 reads=[out_buf, bcst], writes=[out_buf])

        def layer_norm(x_ap, xb, g_ap, b_ap, gbb, out_ap, ob):
            k = stat_i[0] % NST
            stat_i[0] += 1
            for i in range(2):
                P.op(DVE, "bn_stats", st6[:, k, i * 6:(i + 1) * 6], x_ap[:, i * 512:(i + 1) * 512],
                     reads=[xb], writes=[bst[k]])
            P.op(DVE, "bn_aggr", mvt[:, k, :], st6[:, k, :], reads=[bst[k]], writes=[bmv[k]])
            rstd_from(mvt[:, k, 1:2], bmv[k], LN_EPS, 1.0, rsn[:, k, 0:1], brs[k])
            P.op(DVE, "scalar_tensor_tensor", rsn[:, k, 1:2], mvt[:, k, 0:1], -1.0, rsn[:, k, 0:1],
                 ALU.mult, ALU.mult, reads=[bmv[k], brs[k]], writes=[brs[k]])
            P.op(ACT, "activation", x_ap, x_ap, AF.Identity, bias=rsn[:, k, 1:2], scale=rsn[:, k, 0:1],
                 reads=[xb, brs[k]], writes=[xb])
            P.op(POOL, "tensor_tensor", x_ap, x_ap, g_ap, ALU.mult, reads=[xb, gbb], writes=[xb])
            P.op(DVE, "tensor_tensor", out_ap, x_ap, b_ap, ALU.add, reads=[xb, gbb], writes=[ob])

        def transposes(src_ap, sb_, nchunk, bank, dst_ap, db, evac_eng=DVE):
            pv = pbf(bank, 1)[:, 0:nchunk * 128].rearrange("p (c q) -> p c q", c=nchunk)
            wb = [bps[bank]]
            for c in range(nchunk):
                P.op(PE, "transpose", pv[:, c, :], src_ap[:, c * 128:(c + 1) * 128], ident,
                     reads=[sb_, bcmat], writes=wb)
            if evac_eng == ACT:
                P.op(ACT, "copy", dst_ap, pv, reads=wb, writes=[db])
            else:
                P.op(evac_eng, "tensor_copy", dst_ap, pv, reads=wb, writes=[db])

        CA.reset()
        wA = CA.bf16(8 * 1280).rearrange("p (c n) -> p c n", c=8)
        swbias = CA.f32(2048).rearrange("p (h c) -> p h c", h=8)
        swmask = CA.f32(512).rearrange("p (a c) -> p a c", a=2)
        gb_in = CA.f32(2048).rearrange("p (a d) -> p a d", a=2)
        gsw = CA.f32(512)
        xo2 = [CA.f32(1024) for _ in range(2)]
        xp2 = [CA.f32(1024) for _ in range(2)]
        hbf_o = CA.bf16(1024); hbf_p = CA.bf16(1024)
        hT_o = CA.bf16(1024).rearrange("p (c q) -> p c q", c=8)
        hT_p = CA.bf16(1024).rearrange("p (c q) -> p c q", c=8)
        qsT = CA.bf16(512).rearrange("p (a q) -> p a q", a=4)
        ksT = CA.bf16(256)
        vs = CA.bf16(256).rearrange("p (a n) -> p a n", a=2)
        lg = CA.f32(2048).rearrange("p (h c) -> p h c", h=8)
        Pm = CA.bf16(2048).rearrange("p (h c) -> p h c", h=8)
        PT = CA.bf16(2048).rearrange("p (i q) -> p i q", i=16)
        sw32 = CA.f32(512)
        junk = CA.f32(512)
        swbf = CA.bf16(512)
        smallA = CA.f32(64)
        bwA = P.buf("wA"); bswb = P.buf("swbias"); bswm = P.buf("swmask"); bgbin = P.buf("gbin"); bgsw = P.buf("gsw")
        bxo = P.bufs(2, "xo"); bxp = P.bufs(2, "xp")
        bhbo = P.buf(); bhbp = P.buf(); bhTo = P.buf(); bhTp = P.buf()
        bqsT = P.buf(); bksT = P.buf(); bvs = P.buf(); blg = P.buf(); bPm = P.buf(); bPT = P.buf()
        bsw32 = P.buf(); bjunk = P.buf(); bswbf = P.buf(); bsmall = P.buf("smallA")

        stage_state = {"bufs": None, "i": 0}

        def set_stage(aps):
            stage_state["bufs"] = [(ap, P.buf(f"stg{P.nbuf}")) for ap in aps]
            stage_state["i"] = 0

        def wload(dst, src_d, nchunk, ncols, semname, wbuf):
            for c in range(nchunk):
                for c0 in range(0, ncols, 1024):
                    c1 = min(ncols, c0 + 1024)
                    i = stage_state["i"]
                    stage_state["i"] += 1
                    sap, sbuf_ = stage_state["bufs"][i % len(stage_state["bufs"])]
                    P.dma(SP, sap[:, 0:c1 - c0], src_d[c * 128:(c + 1) * 128, c0:c1], f"{semname}{i % len(stage_state['bufs'])}",
                          writes=[sbuf_])
                    eng = DVE if (i % 2 == 0) else POOL
                    P.op(eng, "tensor_copy", dst[:, c, c0:c1], sap[:, 0:c1 - c0], reads=[sbuf_], writes=[wbuf])

        def big_out(dst_d, src_ap, total, piece, reads):
            for c0 in range(0, total, piece):
                P.dma(SP, dst_d[:, c0:c0 + piece], src_ap[:, c0:c0 + piece], "dbg", reads=reads)

        set_stage([arena[:, 24576:25600], arena[:, 25600:26624]])
        wload(wA, wA_d, 8, 1280, "wA", bwA)
        CG.reset()
        wB = CG.bf16(8 * 1024).rearrange("p (c n) -> p c n", c=8)
        bwB = P.buf("wB")
        wload(wB, wB_d, 8, 1024, "wB", bwB)
        P.dma(SP, swbias.rearrange("p h c -> p (h c)"), swbias_d, "cA", writes=[bswb])
        P.dma(SP, swmask.rearrange("p a c -> p (a c)"), swmask_d, "cA", writes=[bswm])
        P.dma(SP, gb_in[:, 0, :], lnin_g.partition_broadcast(128), "cA", writes=[bgbin])
        P.dma(SP, gb_in[:, 1, :], lnin_b.partition_broadcast(128), "cA", writes=[bgbin])
        P.dma(SP, gsw, gsw_d.partition_broadcast(128), "cA", writes=[bgsw])
        P.group_final("cA", [bswb, bswm, bgbin, bgsw])

        swbm = arena[:, 26624:26624 + 4096].rearrange("p (v h c) -> p v h c", v=2, h=8)
        bswbm = P.buf("swbm")
        for v_ in range(2):
            for hh in range(8):
                P.op(DVE, "tensor_tensor", swbm[:, v_, hh, :], swbias[:, hh, :], swmask[:, v_, :], ALU.add,
                     reads=[bswb, bswm], writes=[bswbm])
        mx = smallA[:, 0:8]; mm = smallA[:, 8:16]; negm = smallA[:, 16:24]; rs = smallA[:, 24:32]
        es = smallA[:, 32:40]; den = smallA[:, 40:48]; rinv = smallA[:, 48:56]
        ssq = smallA[:, 56:57]; rr = smallA[:, 57:58]

        def loadA(n):
            k2 = n % 2
            P.dma(SP, xo2[k2], xown[n * 128:(n + 1) * 128, :], f"xo{k2}", writes=[bxo[k2]])
            P.dma(SP, xp2[k2], xprev[n * 128:(n + 1) * 128, :], f"xp{k2}", writes=[bxp[k2]])

        loadA(0)
        for n in range(NSLOT if "A" in phases else 0):
            k2 = n % 2
            xo, xp = xo2[k2], xp2[k2]
            if n + 1 < NSLOT:
                loadA(n + 1)
            layer_norm(xo, bxo[k2], gb_in[:, 0, :], gb_in[:, 1, :], bgbin, hbf_o, bhbo)
            layer_norm(xp, bxp[k2], gb_in[:, 0, :], gb_in[:, 1, :], bgbin, hbf_p, bhbp)
            transposes(hbf_o, bhbo, 8, 0, hT_o, bhTo, DVE)
            transposes(hbf_p, bhbp, 8, 1, hT_p, bhTp, ACT)
            pq = pbank[2].rearrange("p (a q) -> p a q", a=4)
            for p_ in range(4):
                for c in range(8):
                    P.op(PE, "matmul", pq[:, p_, :], wA[:, c, p_ * 128:(p_ + 1) * 128], hT_o[:, c, :],
                         start=(c == 0), stop=(c == 7), reads=[bwA, bhTo], writes=[bps[2]])
            P.op(ACT, "activation", QM[:, n, :, :].rearrange("p a q -> p (a q)"), pbank[2], AF.Identity, scale=0.125,
                 reads=[bps[2]], writes=[bQM[n]])
            pq2 = pbank[3].rearrange("p (a q) -> p a q", a=4)
            for p_ in range(4):
                for c in range(8):
                    P.op(PE, "matmul", pq2[:, p_, :], wA[:, c, 512 + p_ * 128:512 + (p_ + 1) * 128], hT_o[:, c, :],
                         start=(c == 0), stop=(c == 7), reads=[bwA, bhTo], writes=[bps[3]])
            P.op(DVE, "tensor_scalar", qsT.rearrange("p a q -> p (a q)"), pbank[3], 0.125, None, ALU.mult,
                 reads=[bps[3]], writes=[bqsT])
            for (off, hT, bh) in ((0, hT_p, bhTp), (128, hT_o, bhTo)):
                for c in range(8):
                    P.op(PE, "matmul", pbank[4][:, off:off + 128], wA[:, c, 1024:1152], hT[:, c, :],
                         start=(c == 0), stop=(c == 7), reads=[bwA, bh], writes=[bps[4]])
            for (off, hT, bh) in ((256, hT_p, bhTp), (384, hT_o, bhTo)):
                for c in range(8):
                    P.op(PE, "matmul", pbank[4][:, off:off + 128], hT[:, c, :], wA[:, c, 1152:1280],
                         start=(c == 0), stop=(c == 7), reads=[bwA, bh], writes=[bps[4]])
            P.op(ACT, "copy", ksT, pbank[4][:, 0:256], reads=[bps[4]], writes=[bksT])
            P.op(DVE, "tensor_copy", vs.rearrange("p a n -> p (a n)"), pbank[4][:, 256:512],
                 reads=[bps[4]], writes=[bvs])
            mk = 0 if n == 0 else 1
            for r in range(2):
                b0 = 5 if r == 0 else 0
                plg = psum[:, b0:b0 + 2, :].rearrange("p b (a c) -> p (b a) c", a=2)
                wbk = [bps[b0], bps[b0 + 1]]
                for a in range(4):
                    P.op(PE, "matmul", psum[:, b0 + a // 2, (a % 2) * 256:(a % 2 + 1) * 256],
                         qsT[r * 64:(r + 1) * 64, a, :], ksT[r * 64:(r + 1) * 64, :],
                         start=True, stop=True, reads=[bqsT, bksT], writes=[wbk[a // 2]])
                for bb in range(2):
                    hs = 4 * r + 2 * bb
                    P.op(DVE, "tensor_tensor", lg[:, hs:hs + 2, :].rearrange("p h c -> p (h c)"), pbank[b0 + bb],
                         swbm[:, mk, hs:hs + 2, :].rearrange("p h c -> p (h c)"), ALU.add,
                         reads=[wbk[bb], bswbm], writes=[blg])
            for hh in range(8):
                P.op(DVE, "reduce_max", mx[:, hh:hh + 1], lg[:, hh, :], AX.X, reads=[blg], writes=[bsmall])
            P.op(DVE, "tensor_tensor", mm, mx, sinkb[:], ALU.max, reads=[bsmall, bsink], writes=[bsmall])
            P.op(DVE, "tensor_scalar", negm, mm, -1.0, None, ALU.mult, reads=[bsmall], writes=[bsmall])
            for hh in range(8):
                P.op(ACT, "activation", Pm[:, hh, :], lg[:, hh, :], AF.Exp, bias=negm[:, hh:hh + 1],
                     accum_out=rs[:, hh:hh + 1], reads=[blg, bsmall], writes=[bPm, bsmall])
            P.op(DVE, "tensor_tensor", es, sinkb[:], negm, ALU.add, reads=[bsmall, bsink], writes=[bsmall])
            P.op(ACT, "activation", es, es, AF.Exp, reads=[bsmall], writes=[bsmall])
            P.op(DVE, "tensor_tensor", den, rs, es, ALU.add, reads=[bsmall], writes=[bsmall])
            P.op(DVE, "reciprocal", rinv, den, reads=[bsmall], writes=[bsmall])
            ptv = pbf(2, 2).rearrange("p (i q) -> p i q", i=16)
            for hh in range(8):
                for half in range(2):
                    P.op(PE, "transpose", ptv[:, hh * 2 + half, :], Pm[:, hh, half * 128:(half + 1) * 128], ident,
                         reads=[bPm, bcmat], writes=[bps[2], bps[3]])
            P.op(DVE, "tensor_copy", PT[:, 0:8, :].rearrange("p i q -> p (i q)"), pbf(2, 1), reads=[bps[2]], writes=[bPT])
            P.op(ACT, "copy", PT[:, 8:16, :].rearrange("p i q -> p (i q)"), pbf(3, 1), reads=[bps[3]], writes=[bPT])
            for hh in range(8):
                kv = hh // 4
                for half in range(2):
                    P.op(PE, "matmul", pbank[7][:, hh * 64:(hh + 1) * 64], PT[:, hh * 2 + half, :],
                         vs[:, half, kv * 64:(kv + 1) * 64], start=(half == 0), stop=(half == 1),
                         reads=[bPT, bvs], writes=[bps[7]])
            for hh in range(8):
                P.op(DVE, "tensor_scalar", sw32[:, hh * 64:(hh + 1) * 64], pbank[7][:, hh * 64:(hh + 1) * 64],
                     rinv[:, hh:hh + 1], None, ALU.mult, reads=[bps[7], bsmall], writes=[bsw32])
            P.op(ACT, "activation", junk, sw32, AF.Square, accum_out=ssq, reads=[bsw32], writes=[bjunk, bsmall])
            rstd_from(ssq, bsmall, RMS_EPS, 1.0 / 512.0, rr, bsmall)
            P.op(DVE, "scalar_tensor_tensor", swbf, sw32, rr, gsw, ALU.mult, ALU.mult,
                 reads=[bsw32, bsmall, bgsw], writes=[bswbf])
            transposes(swbf, bswbf, 4, 4, MSW[:, n, :, :], bMSW[n], ACT)

        if debug:
            big_out(dbg["qt"], QM.rearrange("p n c q -> p (n c q)"), NSLOT * 512, 2048, bQM)
            big_out(dbg["msw"], MSW.rearrange("p n c q -> p (n c q)"), NSLOT * 512, 2048, bMSW)
        P.barrier()

        CA.reset()
        Kc = CA.bf16(4 * S).rearrange("p (a t) -> p a t", a=4)
        Vc = CA.bf16(NB * 512).rearrange("p (k n) -> p k n", k=NB)
        bK = P.bufs(NB, "K"); bV = P.bufs(NB, "V")
        CG.reset()
        CG.f32(4096)
        gb_in2 = CG.f32(2048).rearrange("p (a d) -> p a d", a=2)
        xb2 = [CG.f32(1024) for _ in range(2)]
        hbfB = CG.bf16(1024)
        hTB = [CG.bf16(1024).rearrange("p (c q) -> p c q", c=8) for _ in range(2)]
        bgb2 = P.buf("gb2"); bxb = P.bufs(2, "xb"); bhbB = P.buf(); bhTB = P.bufs(2, "hTB")
        P.dma(SP, gb_in2[:, 0, :], lnin_g.partition_broadcast(128), "cB", writes=[bgb2])
        P.dma(SP, gb_in2[:, 1, :], lnin_b.partition_broadcast(128), "cB", writes=[bgb2])

        def loadB(t):
            P.dma(SP, xb2[t % 2], xall[t * 128:(t + 1) * 128, :], f"xb{t % 2}", writes=[bxb[t % 2]])

        loadB(0)
        for t in range(NB if "B" in phases else 0):
            k2 = t % 2
            if t + 1 < NB:
                loadB(t + 1)
            layer_norm(xb2[k2], bxb[k2], gb_in2[:, 0, :], gb_in2[:, 1, :], bgb2, hbfB, bhbB)
            transposes(hbfB, bhbB, 8, k2, hTB[k2], bhTB[k2], DVE)
            pk = pbank[2 + k2].rearrange("p (a q) -> p a q", a=4)
            for p_ in range(4):
                for c in range(8):
                    P.op(PE, "matmul", pk[:, p_, :], wB[:, c, p_ * 128:(p_ + 1) * 128], hTB[k2][:, c, :],
                         start=(c == 0), stop=(c == 7), reads=[bwB, bhTB[k2]], writes=[bps[2 + k2]])
            P.op(ACT, "copy", Kc[:, :, t * 128:(t + 1) * 128], pk, reads=[bps[2 + k2]], writes=[bK[t]])
            for c in range(8):
                P.op(PE, "matmul", pbank[4 + k2], hTB[k2][:, c, :], wB[:, c, 512:1024],
                     start=(c == 0), stop=(c == 7), reads=[bwB, bhTB[k2]], writes=[bps[4 + k2]])
            P.op(ACT, "copy", Vc[:, t, :], pbank[4 + k2], reads=[bps[4 + k2]], writes=[bV[t]])
        if debug:
            big_out(dbg["k"], Kc.rearrange("p a t -> p (a t)"), 4 * S, 2048, bK)
            big_out(dbg["v"], Vc.rearrange("p k n -> p (k n)"), NB * 512, 2048, bV)
        P.barrier()

        CG.reset()
        E2 = [CG.f32(512) for _ in range(2)]
        Lp2 = [CG.bf16(512) for _ in range(2)]
        A2 = [CG.bf16(512) for _ in range(2)]
        S32 = [CG.f32(512) for _ in range(2)]
        S16 = [[CG.bf16(512) for _ in range(2)] for _ in range(2)]
        sbmask = CG.bf16(8 * 512).rearrange("p (m q) -> p m q", m=8)
        gsb = CG.f32(512)
        sb32 = CG.f32(512)
        sbbf = CG.bf16(512)
        smallC = CG.f32(8)
        bE = P.bufs(2, "E"); bLp = P.bufs(2, "Lp"); bA = P.bufs(2, "A"); bS32 = P.bufs(2, "S32")
        bS16 = [P.bufs(2, "S16a"), P.bufs(2, "S16b")]
        bsbm = P.buf("sbmask"); bgsb = P.buf("gsb"); bsb32 = P.buf(); bsbbf = P.buf(); bsmC = P.buf()
        P.dma(SP, sbmask.rearrange("p m q -> p (m q)"), sbmask_d, "cC0", writes=[bsbm])
        P.dma(SP, gsb, gsb_d.partition_broadcast(128), "cC1", writes=[bgsb])

        for n in range(NSLOT if "C" in phases else 0):
            g, s_ = n // 2, n % 2
            kmax = 8 * g + 3 + 4 * s_
            ob = 4 + (n % 2)
            for kb in range(kmax, -1, -1):
                cand = kb >= kmax - 3
                midx = s_ * 4 + (kb - (kmax - 3)) if cand else None
                first = kb == kmax
                par = kb % 2
                for hg in range(2):
                    def qk(bank, stop_last):
                        for hh in range(4):
                            hd = 2 * hh + hg
                            pr, a = hg * 64, hh
                            P.op(PE, "matmul", pbank[bank][:, hh * 128:(hh + 1) * 128],
                                 Kc[pr:pr + 64, a, kb * 128:(kb + 1) * 128], QM[pr:pr + 64, n, a, :],
                                 start=(hh == 0), stop=(stop_last and not cand and hh == 3), skip_group_check=True,
                                 reads=[bK[kb], bQM[n]], writes=[bps[bank]])
                        if cand:
                            P.op(PE, "matmul", pbank[bank], ident, sbmask[:, midx, :],
                                 start=False, stop=stop_last, skip_group_check=True,
                                 reads=[bcmat, bsbm], writes=[bps[bank]])
                    zb, lb = hg, 2 + hg
                    qk(zb, True)
                    P.op(ACT, "activation", E2[hg], pbank[zb], AF.Exp, reads=[bps[zb]], writes=[bE[hg]])
                    P.op(ACT, "activation", Lp2[hg], E2[hg], AF.Ln, bias=cst[:, 1:2],
                         reads=[bE[hg], bcst], writes=[bLp[hg]])
                    qk(lb, False)
                    P.op(PE, "matmul", pbank[lb], trineg, Lp2[hg], start=False, stop=first, skip_group_check=True,
                         reads=[bcmat, bLp[hg]], writes=[bps[lb]])
                    if not first:
                        P.op(PE, "matmul", pbank[lb], onesneg, S16[hg][par], start=False, stop=True, skip_group_check=True,
                             reads=[bcmat, bS16[hg][par]], writes=[bps[lb]])
                    P.op(ACT, "activation", A2[hg], pbank[lb], AF.Exp, reads=[bps[lb]], writes=[bA[hg]])
                    for hh in range(4):
                        hd = 2 * hh + hg
                        P.op(PE, "matmul", pbank[ob][:, hd * 64:(hd + 1) * 64], A2[hg][:, hh * 128:(hh + 1) * 128],
                             Vc[:, kb, hd * 64:(hd + 1) * 64], start=(first and hd == 0), stop=(kb == 0 and hd == 7),
                             skip_group_check=True, reads=[bA[hg], bV[kb]], writes=[bps[ob]])
                    if kb > 0:
                        if first:
                            P.op(POOL, "tensor_copy", S32[hg], Lp2[hg], reads=[bLp[hg]], writes=[bS32[hg]])
                        else:
                            P.op(POOL, "tensor_tensor", S32[hg], S32[hg], Lp2[hg], ALU.add,
                                 reads=[bLp[hg], bS32[hg]], writes=[bS32[hg]])
                        P.op(DVE, "tensor_copy", S16[hg][1 - par], S32[hg],
                             reads=[bS32[hg]], writes=[bS16[hg][1 - par]])
            P.op(DVE, "tensor_copy", sb32, pbank[ob], reads=[bps[ob]], writes=[bsb32])
            P.op(ACT, "activation", E2[0], sb32, AF.Square, accum_out=smallC[:, 0:1],
                 reads=[bsb32], writes=[bE[0], bsmC])
            rstd_from(smallC[:, 0:1], bsmC, RMS_EPS, 1.0 / 512.0, smallC[:, 1:2], bsmC)
            P.op(DVE, "scalar_tensor_tensor", sbbf, sb32, smallC[:, 1:2], gsb, ALU.mult, ALU.mult,
                 reads=[bsb32, bsmC, bgsb], writes=[bsbbf])
            transposes(sbbf, bsbbf, 4, 6 + (n % 2), QM[:, n, :, :], bQM[n], DVE)
        if debug:
            big_out(dbg["msb"], QM.rearrange("p n c q -> p (n c q)"), NSLOT * 512, 2048, bQM)
        P.barrier()

        CA.reset()
        wout = CA.bf16(8 * 1024).rearrange("p (c n) -> p c n", c=8)
        gbD = CA.f32(4096).rearrange("p (a d) -> p a d", a=4)
        xd2 = [CA.f32(1024) for _ in range(2)]
        hd2 = [CA.f32(1024) for _ in range(2)]
        r1 = [CA.f32(1024) for _ in range(2)]
        h1o = [CA.f32(1024) for _ in range(2)]
        bwo = P.buf("wout"); bgbD = P.buf("gbD"); bxd = P.bufs(2, "xd"); bhd = P.bufs(2, "hd")
        br1 = P.bufs(2, "r1"); bh1o = P.bufs(2, "h1o")
        bh1d = P.bufs(NSLOT, "h1d")
        set_stage([CA.f32(1024), CA.f32(1024)])
        wload(wout, wout_d, 8, 1024, "wout", bwo)
        for i, src_ in enumerate((lnin_g, lnin_b, ln1_g, ln1_b)):
            P.dma(SP, gbD[:, i, :], src_.partition_broadcast(128), "cD", writes=[bgbD])

        def loadD(n):
            P.dma(SP, xd2[n % 2], xown[n * 128:(n + 1) * 128, :], f"xd{n % 2}", writes=[bxd[n % 2]])

        loadD(0)
        for n in range(NSLOT if "D" in phases else 0):
            k2 = n % 2
            if n + 1 < NSLOT:
                loadD(n + 1)
            layer_norm(xd2[k2], bxd[k2], gbD[:, 0, :], gbD[:, 1, :], bgbD, hd2[k2], bhd[k2])
            b0 = 2 * k2
            pm = psum[:, b0:b0 + 2, :]
            for half in range(2):
                for c in range(8):
                    src_ = QM[:, n, c, :] if c < 4 else MSW[:, n, c - 4, :]
                    sbuf_ = bQM[n] if c < 4 else bMSW[n]
                    P.op(PE, "matmul", pm[:, half, :], src_, wout[:, c, half * 512:(half + 1) * 512],
                         start=(c == 0), stop=(c == 7), reads=[sbuf_, bwo], writes=[bps[b0 + half]])
            for half in range(2):
                P.op(DVE, "scalar_tensor_tensor", r1[k2][:, half * 512:(half + 1) * 512],
                     hd2[k2][:, half * 512:(half + 1) * 512], ALPHA, pm[:, half, :],
                     ALU.mult, ALU.add, reads=[bhd[k2], bps[b0 + half]], writes=[br1[k2]])
            layer_norm(r1[k2], br1[k2], gbD[:, 2, :], gbD[:, 3, :], bgbD, h1o[k2], bh1o[k2])
            P.dma(SP, h1_d[n * 128:(n + 1) * 128, :], h1o[k2], f"h1w{k2}", reads=[bh1o[k2]], writes=[bh1d[n]])
            if debug:
                P.dma(SP, dbg["h1"][n * 128:(n + 1) * 128, :], h1o[k2], "dbg", reads=[bh1o[k2]])
        P.barrier()

        CA.reset()
        wgu = CA.bf16(8 * 2 * DFF).rearrange("p (c n) -> p c n", c=8)
        wdn = CA.bf16(22 * 1024).rearrange("p (f n) -> p f n", f=22)
        CG.reset()
        h1t = [CG.f32(2048).rearrange("p (s d) -> p s d", s=2) for _ in range(2)]
        h1bf = CG.bf16(2048).rearrange("p (s d) -> p s d", s=2)
        h1T = CG.bf16(2048).rearrange("p (c q) -> p c q", c=8)
        actT = CG.bf16(22 * 256).rearrange("p (f q) -> p f q", f=22)
        sg2 = [CG.f32(256) for _ in range(2)]
        CP.reset()
        gbE = CP.f32(2048).rearrange("p (a d) -> p a d", a=2)
        r2 = [CP.f32(1024) for _ in range(2)]
        yo = [CP.f32(1024) for _ in range(2)]
        bwgu = P.buf("wgu"); bwdn = P.buf("wdn"); bh1t = P.bufs(2, "h1t"); bh1bf = P.buf(); bh1T = P.buf()
        bact = P.bufs(22, "act"); bsg = P.bufs(2, "sg"); bgbE = P.buf(); br2 = P.bufs(2, "r2"); byo = P.bufs(2, "yo")
        set_stage([CP.f32(1024), CP.f32(1024)])
        wload(wgu, wgu_d, 8, 2 * DFF, "wgu", bwgu)
        wload(wdn, wdn_d, 22, 1024, "wdn", bwdn)
        P.dma(SP, gbE[:, 0, :], ln2_g.partition_broadcast(128), "cE", writes=[bgbE])
        P.dma(SP, gbE[:, 1, :], ln2_b.partition_broadcast(128), "cE", writes=[bgbE])

        def loadE(T):
            P.dma(SP, h1t[T % 2], h1_d[T * 256:(T + 1) * 256, :].rearrange("(s p) d -> p s d", p=128), f"h1r{T % 2}",
                  reads=[bh1d[2 * T], bh1d[2 * T + 1]], writes=[bh1t[T % 2]])

        loadE(0)
        for T in range(NSLOT // 2 if "E" in phases else 0):
            t2 = T % 2
            if T + 1 < NSLOT // 2:
                loadE(T + 1)
            P.op(POOL, "tensor_copy", h1bf, h1t[t2], reads=[bh1t[t2]], writes=[bh1bf])
            ptv = pbf(0, 2).rearrange("p (c q) -> p c q", c=8)
            for s_ in range(2):
                for c in range(8):
                    P.op(PE, "transpose", ptv[:, c, s_ * 128:(s_ + 1) * 128], h1bf[:, s_, c * 128:(c + 1) * 128], ident,
                         reads=[bh1bf, bcmat], writes=[bps[0], bps[1]])
            P.op(DVE, "tensor_copy", h1T[:, 0:4, :], ptv[:, 0:4, :], reads=[bps[0]], writes=[bh1T])
            P.op(ACT, "copy", h1T[:, 4:8, :], ptv[:, 4:8, :], reads=[bps[1]], writes=[bh1T])
            for f in range(22):
                bk = 2 + (f % 2)
                for (off, col0) in ((0, f * 128), (256, DFF + f * 128)):
                    for c in range(8):
                        P.op(PE, "matmul", pbank[bk][:, off:off + 256], wgu[:, c, col0:col0 + 128], h1T[:, c, :],
                             start=(c == 0), stop=(c == 7), reads=[bwgu, bh1T], writes=[bps[bk]])
                k2 = f % 2
                P.op(ACT, "activation", sg2[k2], pbank[bk][:, 0:256], AF.Silu, reads=[bps[bk]], writes=[bsg[k2]])
                P.op(DVE, "tensor_tensor", actT[:, f, :], sg2[k2], pbank[bk][:, 256:512], ALU.mult,
                     reads=[bsg[k2], bps[bk]], writes=[bact[f]])
            for s_ in range(2):
                n = 2 * T + s_
                b0 = 4 + 2 * s_
                py = psum[:, b0:b0 + 2, :]
                for half in range(2):
                    for f in range(22):
                        P.op(PE, "matmul", py[:, half, :], actT[:, f, s_ * 128:(s_ + 1) * 128],
                             wdn[:, f, half * 512:(half + 1) * 512], start=(f == 0), stop=(f == 21),
                             reads=[bact[f], bwdn], writes=[bps[b0 + half]])
                for half in range(2):
                    P.op(DVE, "scalar_tensor_tensor", r2[s_][:, half * 512:(half + 1) * 512],
                         h1t[t2][:, s_, half * 512:(half + 1) * 512], ALPHA, py[:, half, :],
                         ALU.mult, ALU.add, reads=[bh1t[t2], bps[b0 + half]], writes=[br2[s_]])
                layer_norm(r2[s_], br2[s_], gbE[:, 0, :], gbE[:, 1, :], bgbE, yo[s_], byo[s_])
                P.dma(SP, y_d[n * 128:(n + 1) * 128, :], yo[s_], f"yw{s_}", reads=[byo[s_]])
        P.barrier()
        print("instr counts", {e: P.cnt[e] for e in P.cnt}, "waits", P.nwaits, "sems", len(P.semobj))
        P.emit()
    return nc


def _t5_bucket(dist):
    exact = 16
    d = np.maximum(dist, 0)
    d_f = np.maximum(d, 1).astype(np.float32)
    large = exact + (np.log(d_f / np.float32(exact)) / np.float32(np.log(128.0 / exact)) * np.float32(32 - exact)).astype(np.int32)
    large = np.minimum(large, 31)
    return np.where(d < exact, d, large)


def own_blocks(j):
    out = []
    for g in range(8):
        out += [8 * g + j, 8 * g + 7 - j]
    return out


_NC_CACHE = {}


def kernel(x, ln_in_g, ln_in_b, w_in, sb_norm_g, swa_norm_g, sinks, rel_bias, w_out, ln1_g, ln1_b,
           w_gate_up, w_down, ln2_g, ln2_b, _debug=False, _phases="ABCDE", _limit=None):
    f32 = np.float32
    x = np.asarray(x, f32)
    w_in0 = np.asarray(w_in, f32)[0]
    qsw = w_in0[:, 1536:2048].reshape(D, 8, 64)
    perm = [hh for a in range(4) for hh in (a, a + 4)]
    qsw = qsw[:, perm, :].reshape(D, 512)
    wA = np.ascontiguousarray(np.concatenate([w_in0[:, 0:512], qsw, w_in0[:, 2048:2304]], axis=1))
    wB = np.ascontiguousarray(w_in0[:, 512:1536])
    qi = np.arange(128)[:, None]
    cj = np.arange(256)[None, :]
    dist = qi + 128 - cj
    valid = (dist >= 0) & (dist < 128)
    bucket = _t5_bucket(dist)
    swbias = np.asarray(rel_bias, f32)[bucket]
    swbias = np.ascontiguousarray(swbias.transpose(0, 2, 1)).reshape(128, 8 * 256)
    mask_norm = np.where(valid, 0.0, NEG).astype(f32)
    mask_first = np.where(valid & (cj >= 128), 0.0, NEG).astype(f32)
    ident = np.eye(128, dtype=f32)
    jj = np.arange(128)[:, None]
    ss = np.arange(128)[None, :]
    trineg = np.where(jj >= ss, -1.0, 0.0).astype(f32)
    onesneg = -np.ones((128, 128), f32)
    cmat = np.concatenate([ident, trineg, onesneg], axis=1).astype(ml_dtypes.bfloat16)
    tri_mask = np.where(jj >= ss, NEG, 0.0).astype(f32)
    full_mask = np.full((128, 128), NEG, f32)
    zero_mask = np.zeros((128, 128), f32)

    shared = {
        "wA": wA, "wB": wB, "wout": np.ascontiguousarray(np.asarray(w_out, f32)[0]),
        "wgu": np.ascontiguousarray(np.asarray(w_gate_up, f32)[0]),
        "wdn": np.ascontiguousarray(np.asarray(w_down, f32)[0]),
        "lnin_g": np.asarray(ln_in_g, f32).reshape(1, D), "lnin_b": np.asarray(ln_in_b, f32).reshape(1, D),
        "ln1_g": np.asarray(ln1_g, f32).reshape(1, D), "ln1_b": np.asarray(ln1_b, f32).reshape(1, D),
        "ln2_g": np.asarray(ln2_g, f32).reshape(1, D), "ln2_b": np.asarray(ln2_b, f32).reshape(1, D),
        "gsb": np.asarray(sb_norm_g, f32).reshape(1, 512), "gsw": np.asarray(swa_norm_g, f32).reshape(1, 512),
        "sinks": np.asarray(sinks, f32).reshape(1, 8),
        "swbias": swbias, "cmat": cmat,
    }
    in_maps = []
    for c in range(8):
        b, j = c // 4, c % 4
        blocks = own_blocks(j)
        xo = np.concatenate([x[b, k * 128:(k + 1) * 128] for k in blocks], axis=0)
        xp = np.concatenate([x[b, (k - 1) * 128:k * 128] if k > 0 else np.zeros((128, D), f32) for k in blocks], axis=0)
        swmask = np.concatenate([mask_first if blocks[0] == 0 else mask_norm, mask_norm], axis=1)
        tiles = []
        for s_ in range(2):
            for m in range(4):
                dm = j if s_ == 0 else 3 - j
                tl = zero_mask if m < dm else (tri_mask if m == dm else full_mask)
                tiles.append(np.tile(tl, (1, 4)))
        sbmask = np.concatenate(tiles, axis=1).astype(ml_dtypes.bfloat16)
        d = dict(shared)
        d.update({"xall": np.ascontiguousarray(x[b]), "xown": np.ascontiguousarray(xo), "xprev": np.ascontiguousarray(xp),
                  "swmask": np.ascontiguousarray(swmask), "sbmask": np.ascontiguousarray(sbmask)})
        in_maps.append(d)

    key = (bool(_debug), _phases, _limit)
    if key not in _NC_CACHE:
        _NC_CACHE[key] = build(debug=_debug, phases=_phases, limit=_limit)
    nc = _NC_CACHE[key]
    res = run_bass_kernel_spmd(nc, in_maps, core_ids=list(range(8)))
    out = np.zeros((2, S, D), f32)
    for c in range(8):
        b, j = c // 4, c % 4
        yc = res.results[c]["y"]
        for n, k in enumerate(own_blocks(j)):
            out[b, k * 128:(k + 1) * 128] = yc[n * 128:(n + 1) * 128]
    if _debug:
        return out, res.results
    return out
```

```python
from contextlib import ExitStack
import numpy as np
import ml_dtypes
import concourse.bass as bass
import concourse.mybir as mybir
from concourse.bass_utils import run_bass_kernel_spmd

F32 = mybir.dt.float32
BF16 = mybir.dt.bfloat16
AF = mybir.ActivationFunctionType
ALU = mybir.AluOpType
AX = mybir.AxisListType

PE, ACT, DVE, POOL, SP = "tensor", "scalar", "vector", "gpsimd", "sync"
ENGS = (PE, ACT, DVE, POOL, SP)

D = 1024
S = 8192
NB = 64
DFF = 2816
NSLOT = 16
ALPHA = 2.0 ** 0.25
LN_EPS = 1e-5
RMS_EPS = 1e-6
NEG = -30000.0


class Buf:
    __slots__ = ("name", "last_w", "readers", "excl")

    def __init__(self, name):
        self.name = name
        self.last_w = None
        self.readers = []
        self.excl = False


class Prog:
    def __init__(self, nc, stack, same_engine_sync=True):
        self.nc = nc
        self.stack = stack
        self.same = same_engine_sync
        self.streams = {e: [] for e in ENGS}
        self.sem = {}
        self.cnt = {}
        self.semobj = {}
        for e in (PE, ACT, DVE, POOL):
            self.sem[e] = self._newsem("c_" + e)
            self.cnt[e] = 0
        self.waited = {e: {} for e in ENGS}
        self.dma_sems = {}
        self.dma_cnt = {}
        self.nbuf = 0
        self.nwaits = 0
        self.limit = None
        self.nops = 0

    def _newsem(self, name):
        s = self.stack.enter_context(self.nc.semaphore(name))
        key = len(self.semobj)
        self.semobj[key] = s
        return key

    def buf(self, name=None):
        self.nbuf += 1
        return Buf(name or f"b{self.nbuf}")

    def bufs(self, n, name="b"):
        return [self.buf(f"{name}{i}") for i in range(n)]

    def _wait(self, eng, s, v):
        if self.waited[eng].get(s, 0) < v:
            self.waited[eng][s] = v
            so = self.semobj[s]
            self.nwaits += 1
            self.streams[eng].append(lambda h, so=so, v=v: h.wait_ge(so, v))

    def _deps(self, eng, reads, writes):
        deps = {}

        def add(tok):
            if tok is None:
                return
            s, v = tok
            if deps.get(s, 0) < v:
                deps[s] = v
        for b in reads:
            add(b.last_w)
        for b in writes:
            add(b.last_w)
            for t in b.readers:
                add(t)
        own = self.sem.get(eng)
        for s, v in deps.items():
            if s == own and (eng == PE or not self.same):
                continue
            self._wait(eng, s, v)

    def _commit(self, tok, reads, writes):
        for b in reads:
            b.readers.append(tok)
        for b in writes:
            b.last_w = tok
            b.readers = []

    def op(self, eng, name, *args, reads=(), writes=(), **kw):
        ex = [b for b in reads if b.excl]
        if ex:
            reads = [b for b in reads if not b.excl]
            writes = list(writes) + ex
        self.nops += 1
        if self.limit is not None and self.nops > self.limit:
            return None
        self._deps(eng, reads, writes)
        self.cnt[eng] += 1
        tok = (self.sem[eng], self.cnt[eng])
        so = self.semobj[self.sem[eng]]
        self.streams[eng].append(
            lambda h, name=name, args=args, kw=kw, so=so: getattr(h, name)(*args, **kw).then_inc(so, 1))
        self._commit(tok, reads, writes)
        return tok

    def group_final(self, semname, bufs):
        tok = (self.dma_sems[semname], self.dma_cnt[semname])
        for b in bufs:
            b.last_w = tok

    def dma(self, q, out, in_, semname, reads=(), writes=()):
        if semname not in self.dma_sems:
            self.dma_sems[semname] = self._newsem("d_" + semname)
            self.dma_cnt[semname] = 0
        s = self.dma_sems[semname]
        if self.limit is not None and self.nops > self.limit:
            return None
        self._deps(q, reads, writes)
        self.dma_cnt[semname] += 16
        tok = (s, self.dma_cnt[semname])
        so = self.semobj[s]
        self.streams[q].append(
            lambda h, out=out, in_=in_, so=so: h.dma_start(out=out, in_=in_).then_inc(so, 16))
        self._commit(tok, reads, writes)
        return tok

    def barrier(self):
        toks = [(self.sem[e], self.cnt[e]) for e in (PE, ACT, DVE, POOL) if self.cnt[e] > 0]
        toks += [(self.dma_sems[n], self.dma_cnt[n]) for n in self.dma_sems if self.dma_cnt[n] > 0]
        for e in ENGS:
            for (s, v) in toks:
                self._wait(e, s, v)

    def emit(self):
        with self.nc.Block() as block:
            def mk(e):
                def body(h):
                    for f in self.streams[e]:
                        f(h)
                return body
            block.tensor(mk(PE))
            block.scalar(mk(ACT))
            block.vector(mk(DVE))
            block.gpsimd(mk(POOL))
            block.sync(mk(SP))


def build(debug=False, same_engine_sync=True, phases="ABCDE", limit=None):
    nc = bass.Bass("TRN2", target_bir_lowering=False)

    def din(name, shape, dt=F32):
        return nc.dram_tensor(name, list(shape), dt, kind="ExternalInput").ap()

    xall = din("xall", [S, D])
    xown = din("xown", [NSLOT * 128, D])
    xprev = din("xprev", [NSLOT * 128, D])
    wA_d = din("wA", [D, 1280])
    wB_d = din("wB", [D, 1024])
    wout_d = din("wout", [D, D])
    wgu_d = din("wgu", [D, 2 * DFF])
    wdn_d = din("wdn", [DFF, D])
    lnin_g = din("lnin_g", [1, D]); lnin_b = din("lnin_b", [1, D])
    ln1_g = din("ln1_g", [1, D]); ln1_b = din("ln1_b", [1, D])
    ln2_g = din("ln2_g", [1, D]); ln2_b = din("ln2_b", [1, D])
    gsb_d = din("gsb", [1, 512]); gsw_d = din("gsw", [1, 512])
    sinks_d = din("sinks", [1, 8])
    swbias_d = din("swbias", [128, 8 * 256])
    swmask_d = din("swmask", [128, 2 * 256])
    sbmask_d = din("sbmask", [128, 8 * 512], BF16)
    cmat_d = din("cmat", [128, 3 * 128], BF16)
    y_d = nc.dram_tensor("y", [NSLOT * 128, D], F32, kind="ExternalOutput").ap()
    h1_d = nc.dram_tensor("h1scr", [NSLOT * 128, D], F32, kind="ExternalOutput").ap()
    dbg = {}
    if debug:
        dbg["k"] = nc.dram_tensor("dbg_k", [128, 4 * S], BF16, kind="ExternalOutput").ap()
        dbg["v"] = nc.dram_tensor("dbg_v", [128, NB * 512], BF16, kind="ExternalOutput").ap()
        dbg["qt"] = nc.dram_tensor("dbg_qt", [128, NSLOT * 512], BF16, kind="ExternalOutput").ap()
        dbg["msw"] = nc.dram_tensor("dbg_msw", [128, NSLOT * 512], BF16, kind="ExternalOutput").ap()
        dbg["msb"] = nc.dram_tensor("dbg_msb", [128, NSLOT * 512], BF16, kind="ExternalOutput").ap()
        dbg["h1"] = nc.dram_tensor("dbg_h1", [NSLOT * 128, D], F32, kind="ExternalOutput").ap()

    with ExitStack() as st:
        P = Prog(nc, st, same_engine_sync)
        P.limit = limit
        sbt = lambda name, shape, dt: st.enter_context(nc.sbuf_tensor(name, list(shape), dt))

        ARW = 33792
        arena = sbt("arena", [128, ARW], F32)
        GENW = 9728
        gen = sbt("gen", [128, GENW], F32)
        persist = sbt("persist", [128, 8192], F32)
        cmat = sbt("cmat_sb", [128, 3, 128], BF16)
        cst = sbt("cst", [128, 4], F32)
        sinkb = sbt("sinkb", [128, 8], F32)
        NST = 4
        st6 = sbt("st6", [128, NST, 12], F32)
        mvt = sbt("mvt", [128, NST, 2], F32)
        rsn = sbt("rsn", [128, NST, 2], F32)
        psum = st.enter_context(nc.psum_tensor("ps", [128, 8, 512], F32))

        ident = cmat[:, 0, :]
        trineg = cmat[:, 1, :]
        onesneg = cmat[:, 2, :]

        class Carver:
            def __init__(self, t, size):
                self.t, self.size, self.off = t, size, 0

            def reset(self):
                self.off = 0

            def f32(self, words):
                a = self.t[:, self.off:self.off + words]
                self.off += words
                assert self.off <= self.size, (self.off, self.size)
                return a

            def bf16(self, elems):
                assert elems % 2 == 0
                return self.f32(elems // 2).bitcast(BF16)

        CA = Carver(arena, ARW)
        CG = Carver(gen, GENW)
        CP = Carver(persist, 8192)

        QM = CP.bf16(NSLOT * 512).rearrange("p (n c q) -> p n c q", n=NSLOT, c=4)
        MSW = CP.bf16(NSLOT * 512).rearrange("p (n c q) -> p n c q", n=NSLOT, c=4)
        bQM = P.bufs(NSLOT, "qm")
        bMSW = P.bufs(NSLOT, "msw")

        pbank = [psum[:, i, :] for i in range(8)]
        bps = P.bufs(8, "ps")
        for b_ in bps:
            b_.excl = True

        def pbf(i, nbanks=1):
            return psum[:, i:i + nbanks, :].rearrange("p b n -> p (b n)").bitcast(BF16)

        bcmat = P.buf("cmat"); bcst = P.buf("cst"); bsink = P.buf("sink")
        bst = P.bufs(NST, "st"); bmv = P.bufs(NST, "mv"); brs = P.bufs(NST, "rs")
        stat_i = [0]

        P.dma(SP, cmat[:].rearrange("p a b -> p (a b)"), cmat_d, "const0", writes=[bcmat])
        P.dma(SP, sinkb[:], sinks_d.partition_broadcast(128), "const1", writes=[bsink])
        P.op(DVE, "memset", cst[:, 0:1], -0.5, writes=[bcst])
        P.op(DVE, "memset", cst[:, 1:2], 1.0, writes=[bcst])
        P.op(DVE, "memset", cst[:, 2:4], 0.0, writes=[bcst])

        def rstd_from(var_ap, var_buf, eps, scale, out_ap, out_buf):
            P.op(DVE, "tensor_scalar", out_ap, var_ap, scale, eps, ALU.mult, ALU.add,
                 reads=[var_buf], writes=[out_buf])
            P.op(POOL, "tensor_tensor", out_ap, out_ap, cst[:, 0:1], ALU.pow,
                 reads=[out_buf, bcst], writes=[out_buf])

        def layer_norm(x_ap, xb, g_ap, b_ap, gbb, out_ap, ob):
            k = stat_i[0] % NST
            stat_i[0] += 1
            for i in range(2):
                P.op(DVE, "bn_stats", st6[:, k, i * 6:(i + 1) * 6], x_ap[:, i * 512:(i + 1) * 512],
                     reads=[xb], writes=[bst[k]])
            P.op(DVE, "bn_aggr", mvt[:, k, :], st6[:, k, :], reads=[bst[k]], writes=[bmv[k]])
            rstd_from(mvt[:, k, 1:2], bmv[k], LN_EPS, 1.0, rsn[:, k, 0:1], brs[k])
            P.op(DVE, "scalar_tensor_tensor", rsn[:, k, 1:2], mvt[:, k, 0:1], -1.0, rsn[:, k, 0:1],
                 ALU.mult, ALU.mult, reads=[bmv[k], brs[k]], writes=[brs[k]])
            P.op(ACT, "activation", x_ap, x_ap, AF.Identity, bias=rsn[:, k, 1:2], scale=rsn[:, k, 0:1],
                 reads=[xb, brs[k]], writes=[xb])
            P.op(POOL, "tensor_tensor", x_ap, x_ap, g_ap, ALU.mult, reads=[xb, gbb], writes=[xb])
            P.op(DVE, "tensor_tensor", out_ap, x_ap, b_ap, ALU.add, reads=[xb, gbb], writes=[ob])

        def transposes(src_ap, sb_, nchunk, bank, dst_ap, db, evac_eng=DVE):
            pv = pbf(bank, 1)[:, 0:nchunk * 128].rearrange("p (c q) -> p c q", c=nchunk)
            wb = [bps[bank]]
            for c in range(nchunk):
                P.op(PE, "transpose", pv[:, c, :], src_ap[:, c * 128:(c + 1) * 128], ident,
                     reads=[sb_, bcmat], writes=wb)
            if evac_eng == ACT:
                P.op(ACT, "copy", dst_ap, pv, reads=wb, writes=[db])
            else:
                P.op(evac_eng, "tensor_copy", dst_ap, pv, reads=wb, writes=[db])

        CA.reset()
        wA = CA.bf16(8 * 1280).rearrange("p (c n) -> p c n", c=8)
        swbias = CA.f32(2048).rearrange("p (h c) -> p h c", h=8)
        swmask = CA.f32(512).rearrange("p (a c) -> p a c", a=2)
        gb_in = CA.f32(2048).rearrange("p (a d) -> p a d", a=2)
        gsw = CA.f32(512)
        xo2 = [CA.f32(1024) for _ in range(2)]
        xp2 = [CA.f32(1024) for _ in range(2)]
        hbf_o = CA.bf16(1024); hbf_p = CA.bf16(1024)
        hT_o = CA.bf16(1024).rearrange("p (c q) -> p c q", c=8)
        hT_p = CA.bf16(1024).rearrange("p (c q) -> p c q", c=8)
        qsT = CA.bf16(512).rearrange("p (a q) -> p a q", a=4)
        ksT = CA.bf16(256)
        vs = CA.bf16(256).rearrange("p (a n) -> p a n", a=2)
        lg = CA.f32(2048).rearrange("p (h c) -> p h c", h=8)
        Pm = CA.bf16(2048).rearrange("p (h c) -> p h c", h=8)
        PT = CA.bf16(2048).rearrange("p (i q) -> p i q", i=16)
        sw32 = CA.f32(512)
        junk = CA.f32(512)
        swbf = CA.bf16(512)
        smallA = CA.f32(64)
        bwA = P.buf("wA"); bswb = P.buf("swbias"); bswm = P.buf("swmask"); bgbin = P.buf("gbin"); bgsw = P.buf("gsw")
        bxo = P.bufs(2, "xo"); bxp = P.bufs(2, "xp")
        bhbo = P.buf(); bhbp = P.buf(); bhTo = P.buf(); bhTp = P.buf()
        bqsT = P.buf(); bksT = P.buf(); bvs = P.buf(); blg = P.buf(); bPm = P.buf(); bPT = P.buf()
        bsw32 = P.buf(); bjunk = P.buf(); bswbf = P.buf(); bsmall = P.buf("smallA")

        stage_state = {"bufs": None, "i": 0}

        def set_stage(aps):
            stage_state["bufs"] = [(ap, P.buf(f"stg{P.nbuf}")) for ap in aps]
            stage_state["i"] = 0

        def wload(dst, src_d, nchunk, ncols, semname, wbuf):
            for c in range(nchunk):
                for c0 in range(0, ncols, 1024):
                    c1 = min(ncols, c0 + 1024)
                    i = stage_state["i"]
                    stage_state["i"] += 1
                    sap, sbuf_ = stage_state["bufs"][i % len(stage_state["bufs"])]
                    P.dma(SP, sap[:, 0:c1 - c0], src_d[c * 128:(c + 1) * 128, c0:c1], f"{semname}{i % len(stage_state['bufs'])}",
                          writes=[sbuf_])
                    eng = DVE if (i % 2 == 0) else POOL
                    P.op(eng, "tensor_copy", dst[:, c, c0:c1], sap[:, 0:c1 - c0], reads=[sbuf_], writes=[wbuf])

        def big_out(dst_d, src_ap, total, piece, reads):
            for c0 in range(0, total, piece):
                P.dma(SP, dst_d[:, c0:c0 + piece], src_ap[:, c0:c0 + piece], "dbg", reads=reads)

        set_stage([arena[:, 24576:25600], arena[:, 25600:26624]])
        wload(wA, wA_d, 8, 1280, "wA", bwA)
        CG.reset()
        wB = CG.bf16(8 * 1024).rearrange("p (c n) -> p c n", c=8)
        bwB = P.buf("wB")
        wload(wB, wB_d, 8, 1024, "wB", bwB)
        P.dma(SP, swbias.rearrange("p h c -> p (h c)"), swbias_d, "cA", writes=[bswb])
        P.dma(SP, swmask.rearrange("p a c -> p (a c)"), swmask_d, "cA", writes=[bswm])
        P.dma(SP, gb_in[:, 0, :], lnin_g.partition_broadcast(128), "cA", writes=[bgbin])
        P.dma(SP, gb_in[:, 1, :], lnin_b.partition_broadcast(128), "cA", writes=[bgbin])
        P.dma(SP, gsw, gsw_d.partition_broadcast(128), "cA", writes=[bgsw])
        P.group_final("cA", [bswb, bswm, bgbin, bgsw])

        swbm = arena[:, 26624:26624 + 4096].rearrange("p (v h c) -> p v h c", v=2, h=8)
        bswbm = P.buf("swbm")
        for v_ in range(2):
            for hh in range(8):
                P.op(DVE, "tensor_tensor", swbm[:, v_, hh, :], swbias[:, hh, :], swmask[:, v_, :], ALU.add,
                     reads=[bswb, bswm], writes=[bswbm])
        mx = smallA[:, 0:8]; mm = smallA[:, 8:16]; negm = smallA[:, 16:24]; rs = smallA[:, 24:32]
        es = smallA[:, 32:40]; den = smallA[:, 40:48]; rinv = smallA[:, 48:56]
        ssq = smallA[:, 56:57]; rr = smallA[:, 57:58]

        def loadA(n):
            k2 = n % 2
            P.dma(SP, xo2[k2], xown[n * 128:(n + 1) * 128, :], f"xo{k2}", writes=[bxo[k2]])
            P.dma(SP, xp2[k2], xprev[n * 128:(n + 1) * 128, :], f"xp{k2}", writes=[bxp[k2]])

        loadA(0)
        for n in range(NSLOT if "A" in phases else 0):
            k2 = n % 2
            xo, xp = xo2[k2], xp2[k2]
            if n + 1 < NSLOT:
                loadA(n + 1)
            layer_norm(xo, bxo[k2], gb_in[:, 0, :], gb_in[:, 1, :], bgbin, hbf_o, bhbo)
            layer_norm(xp, bxp[k2], gb_in[:, 0, :], gb_in[:, 1, :], bgbin, hbf_p, bhbp)
            transposes(hbf_o, bhbo, 8, 0, hT_o, bhTo, DVE)
            transposes(hbf_p, bhbp, 8, 1, hT_p, bhTp, ACT)
            pq = pbank[2].rearrange("p (a q) -> p a q", a=4)
            for p_ in range(4):
                for c in range(8):
                    P.op(PE, "matmul", pq[:, p_, :], wA[:, c, p_ * 128:(p_ + 1) * 128], hT_o[:, c, :],
                         start=(c == 0), stop=(c == 7), reads=[bwA, bhTo], writes=[bps[2]])
            P.op(ACT, "activation", QM[:, n, :, :].rearrange("p a q -> p (a q)"), pbank[2], AF.Identity, scale=0.125,
                 reads=[bps[2]], writes=[bQM[n]])
            pq2 = pbank[3].rearrange("p (a q) -> p a q", a=4)
            for p_ in range(4):
                for c in range(8):
                    P.op(PE, "matmul", pq2[:, p_, :], wA[:, c, 512 + p_ * 128:512 + (p_ + 1) * 128], hT_o[:, c, :],
                         start=(c == 0), stop=(c == 7), reads=[bwA, bhTo], writes=[bps[3]])
            P.op(DVE, "tensor_scalar", qsT.rearrange("p a q -> p (a q)"), pbank[3], 0.125, None, ALU.mult,
                 reads=[bps[3]], writes=[bqsT])
            for (off, hT, bh) in ((0, hT_p, bhTp), (128, hT_o, bhTo)):
                for c in range(8):
                    P.op(PE, "matmul", pbank[4][:, off:off + 128], wA[:, c, 1024:1152], hT[:, c, :],
                         start=(c == 0), stop=(c == 7), reads=[bwA, bh], writes=[bps[4]])
            for (off, hT, bh) in ((256, hT_p, bhTp), (384, hT_o, bhTo)):
                for c in range(8):
                    P.op(PE, "matmul", pbank[4][:, off:off + 128], hT[:, c, :], wA[:, c, 1152:1280],
                         start=(c == 0), stop=(c == 7), reads=[bwA, bh], writes=[bps[4]])
            P.op(ACT, "copy", ksT, pbank[4][:, 0:256], reads=[bps[4]], writes=[bksT])
            P.op(DVE, "tensor_copy", vs.rearrange("p a n -> p (a n)"), pbank[4][:, 256:512],
                 reads=[bps[4]], writes=[bvs])
            mk = 0 if n == 0 else 1
            for r in range(2):
                b0 = 5 if r == 0 else 0
                plg = psum[:, b0:b0 + 2, :].rearrange("p b (a c) -> p (b a) c", a=2)
                wbk = [bps[b0], bps[b0 + 1]]
                for a in range(4):
                    P.op(PE, "matmul", psum[:, b0 + a // 2, (a % 2) * 256:(a % 2 + 1) * 256],
                         qsT[r * 64:(r + 1) * 64, a, :], ksT[r * 64:(r + 1) * 64, :],
                         start=True, stop=True, reads=[bqsT, bksT], writes=[wbk[a // 2]])
                for bb in range(2):
                    hs = 4 * r + 2 * bb
                    P.op(DVE, "tensor_tensor", lg[:, hs:hs + 2, :].rearrange("p h c -> p (h c)"), pbank[b0 + bb],
                         swbm[:, mk, hs:hs + 2, :].rearrange("p h c -> p (h c)"), ALU.add,
                         reads=[wbk[bb], bswbm], writes=[blg])
            for hh in range(8):
                P.op(DVE, "reduce_max", mx[:, hh:hh + 1], lg[:, hh, :], AX.X, reads=[blg], writes=[bsmall])
            P.op(DVE, "tensor_tensor", mm, mx, sinkb[:], ALU.max, reads=[bsmall, bsink], writes=[bsmall])
            P.op(DVE, "tensor_scalar", negm, mm, -1.0, None, ALU.mult, reads=[bsmall], writes=[bsmall])
            for hh in range(8):
                P.op(ACT, "activation", Pm[:, hh, :], lg[:, hh, :], AF.Exp, bias=negm[:, hh:hh + 1],
                     accum_out=rs[:, hh:hh + 1], reads=[blg, bsmall], writes=[bPm, bsmall])
            P.op(DVE, "tensor_tensor", es, sinkb[:], negm, ALU.add, reads=[bsmall, bsink], writes=[bsmall])
            P.op(ACT, "activation", es, es, AF.Exp, reads=[bsmall], writes=[bsmall])
            P.op(DVE, "tensor_tensor", den, rs, es, ALU.add, reads=[bsmall], writes=[bsmall])
            P.op(DVE, "reciprocal", rinv, den, reads=[bsmall], writes=[bsmall])
            ptv = pbf(2, 2).rearrange("p (i q) -> p i q", i=16)
            for hh in range(8):
                for half in range(2):
                    P.op(PE, "transpose", ptv[:, hh * 2 + half, :], Pm[:, hh, half * 128:(half + 1) * 128], ident,
                         reads=[bPm, bcmat], writes=[bps[2], bps[3]])
            P.op(DVE, "tensor_copy", PT[:, 0:8, :].rearrange("p i q -> p (i q)"), pbf(2, 1), reads=[bps[2]], writes=[bPT])
            P.op(ACT, "copy", PT[:, 8:16, :].rearrange("p i q -> p (i q)"), pbf(3, 1), reads=[bps[3]], writes=[bPT])
            for hh in range(8):
                kv = hh // 4
                for half in range(2):
                    P.op(PE, "matmul", pbank[7][:, hh * 64:(hh + 1) * 64], PT[:, hh * 2 + half, :],
                         vs[:, half, kv * 64:(kv + 1) * 64], start=(half == 0), stop=(half == 1),
                         reads=[bPT, bvs], writes=[bps[7]])
            for hh in range(8):
                P.op(DVE, "tensor_scalar", sw32[:, hh * 64:(hh + 1) * 64], pbank[7][:, hh * 64:(hh + 1) * 64],
                     rinv[:, hh:hh + 1], None, ALU.mult, reads=[bps[7], bsmall], writes=[bsw32])
            P.op(ACT, "activation", junk, sw32, AF.Square, accum_out=ssq, reads=[bsw32], writes=[bjunk, bsmall])
            rstd_from(ssq, bsmall, RMS_EPS, 1.0 / 512.0, rr, bsmall)
            P.op(DVE, "scalar_tensor_tensor", swbf, sw32, rr, gsw, ALU.mult, ALU.mult,
                 reads=[bsw32, bsmall, bgsw], writes=[bswbf])
            transposes(swbf, bswbf, 4, 4, MSW[:, n, :, :], bMSW[n], ACT)

        if debug:
            big_out(dbg["qt"], QM.rearrange("p n c q -> p (n c q)"), NSLOT * 512, 2048, bQM)
            big_out(dbg["msw"], MSW.rearrange("p n c q -> p (n c q)"), NSLOT * 512, 2048, bMSW)
        P.barrier()

        CA.reset()
        Kc = CA.bf16(4 * S).rearrange("p (a t) -> p a t", a=4)
        Vc = CA.bf16(NB * 512).rearrange("p (k n) -> p k n", k=NB)
        bK = P.bufs(NB, "K"); bV = P.bufs(NB, "V")
        CG.reset()
        CG.f32(4096)
        gb_in2 = CG.f32(2048).rearrange("p (a d) -> p a d", a=2)
        xb2 = [CG.f32(1024) for _ in range(2)]
        hbfB = CG.bf16(1024)
        hTB = [CG.bf16(1024).rearrange("p (c q) -> p c q", c=8) for _ in range(2)]
        bgb2 = P.buf("gb2"); bxb = P.bufs(2, "xb"); bhbB = P.buf(); bhTB = P.bufs(2, "hTB")
        P.dma(SP, gb_in2[:, 0, :], lnin_g.partition_broadcast(128), "cB", writes=[bgb2])
        P.dma(SP, gb_in2[:, 1, :], lnin_b.partition_broadcast(128), "cB", writes=[bgb2])

        def loadB(t):
            P.dma(SP, xb2[t % 2], xall[t * 128:(t + 1) * 128, :], f"xb{t % 2}", writes=[bxb[t % 2]])

        loadB(0)
        for t in range(NB if "B" in phases else 0):
            k2 = t % 2
            if t + 1 < NB:
                loadB(t + 1)
            layer_norm(xb2[k2], bxb[k2], gb_in2[:, 0, :], gb_in2[:, 1, :], bgb2, hbfB, bhbB)
            transposes(hbfB, bhbB, 8, k2, hTB[k2], bhTB[k2], DVE)
            pk = pbank[2 + k2].rearrange("p (a q) -> p a q", a=4)
            for p_ in range(4):
                for c in range(8):
                    P.op(PE, "matmul", pk[:, p_, :], wB[:, c, p_ * 128:(p_ + 1) * 128], hTB[k2][:, c, :],
                         start=(c == 0), stop=(c == 7), reads=[bwB, bhTB[k2]], writes=[bps[2 + k2]])
            P.op(ACT, "copy", Kc[:, :, t * 128:(t + 1) * 128], pk, reads=[bps[2 + k2]], writes=[bK[t]])
            for c in range(8):
                P.op(PE, "matmul", pbank[4 + k2], hTB[k2][:, c, :], wB[:, c, 512:1024],
                     start=(c == 0), stop=(c == 7), reads=[bwB, bhTB[k2]], writes=[bps[4 + k2]])
            P.op(ACT, "copy", Vc[:, t, :], pbank[4 + k2], reads=[bps[4 + k2]], writes=[bV[t]])
        if debug:
            big_out(dbg["k"], Kc.rearrange("p a t -> p (a t)"), 4 * S, 2048, bK)
            big_out(dbg["v"], Vc.rearrange("p k n -> p (k n)"), NB * 512, 2048, bV)
        P.barrier()

        CG.reset()
        E2 = [CG.f32(512) for _ in range(2)]
        Lp2 = [CG.bf16(512) for _ in range(2)]
        A2 = [CG.bf16(512) for _ in range(2)]
        S32 = [CG.f32(512) for _ in range(2)]
        S16 = [[CG.bf16(512) for _ in range(2)] for _ in range(2)]
        sbmask = CG.bf16(8 * 512).rearrange("p (m q) -> p m q", m=8)
        gsb = CG.f32(512)
        sb32 = CG.f32(512)
        sbbf = CG.bf16(512)
        smallC = CG.f32(8)
        bE = P.bufs(2, "E"); bLp = P.bufs(2, "Lp"); bA = P.bufs(2, "A"); bS32 = P.bufs(2, "S32")
        bS16 = [P.bufs(2, "S16a"), P.bufs(2, "S16b")]
        bsbm = P.buf("sbmask"); bgsb = P.buf("gsb"); bsb32 = P.buf(); bsbbf = P.buf(); bsmC = P.buf()
        P.dma(SP, sbmask.rearrange("p m q -> p (m q)"), sbmask_d, "cC0", writes=[bsbm])
        P.dma(SP, gsb, gsb_d.partition_broadcast(128), "cC1", writes=[bgsb])

        for n in range(NSLOT if "C" in phases else 0):
            g, s_ = n // 2, n % 2
            kmax = 8 * g + 3 + 4 * s_
            ob = 4 + (n % 2)

            def qk(bank, hg, kb, stop_last):
                cand = kb >= kmax - 3
                for hh in range(4):
                    pr, a = hg * 64, hh
                    P.op(PE, "matmul", pbank[bank][:, hh * 128:(hh + 1) * 128],
                         Kc[pr:pr + 64, a, kb * 128:(kb + 1) * 128], QM[pr:pr + 64, n, a, :],
                         start=(hh == 0), stop=(stop_last and not cand and hh == 3), skip_group_check=True,
                         reads=[bK[kb], bQM[n]], writes=[bps[bank]])
                if cand:
                    midx = s_ * 4 + (kb - (kmax - 3))
                    P.op(PE, "matmul", pbank[bank], ident, sbmask[:, midx, :],
                         start=False, stop=stop_last, skip_group_check=True,
                         reads=[bcmat, bsbm], writes=[bps[bank]])

            for hg in range(2):
                qk(hg, hg, kmax, True)
            for kb in range(kmax, -1, -1):
                first = kb == kmax
                par = kb % 2
                for hg in range(2):
                    P.op(ACT, "activation", E2[hg], pbank[hg], AF.Exp, reads=[bps[hg]], writes=[bE[hg]])
                    P.op(ACT, "activation", Lp2[hg], E2[hg], AF.Ln, bias=cst[:, 1:2],
                         reads=[bE[hg], bcst], writes=[bLp[hg]])
                for hg in range(2):
                    lb = 2 + hg
                    qk(lb, hg, kb, False)
                    P.op(PE, "matmul", pbank[lb], trineg, Lp2[hg], start=False, stop=first, skip_group_check=True,
                         reads=[bcmat, bLp[hg]], writes=[bps[lb]])
                    if not first:
                        P.op(PE, "matmul", pbank[lb], onesneg, S16[hg][par], start=False, stop=True, skip_group_check=True,
                             reads=[bcmat, bS16[hg][par]], writes=[bps[lb]])
                if kb > 0:
                    for hg in range(2):
                        qk(hg, hg, kb - 1, True)
                for hg in range(2):
                    P.op(ACT, "activation", A2[hg], pbank[2 + hg], AF.Exp, reads=[bps[2 + hg]], writes=[bA[hg]])
                for hg in range(2):
                    for hh in range(4):
                        hd = 2 * hh + hg
                        P.op(PE, "matmul", pbank[ob][:, hd * 64:(hd + 1) * 64], A2[hg][:, hh * 128:(hh + 1) * 128],
                             Vc[:, kb, hd * 64:(hd + 1) * 64], start=(first and hd == 0), stop=(kb == 0 and hd == 7),
                             skip_group_check=True, reads=[bA[hg], bV[kb]], writes=[bps[ob]])
                if kb > 0:
                    for hg in range(2):
                        if first:
                            P.op(POOL, "tensor_copy", S32[hg], Lp2[hg], reads=[bLp[hg]], writes=[bS32[hg]])
                        else:
                            P.op(POOL, "tensor_tensor", S32[hg], S32[hg], Lp2[hg], ALU.add,
                                 reads=[bLp[hg], bS32[hg]], writes=[bS32[hg]])
                        P.op(DVE, "tensor_copy", S16[hg][1 - par], S32[hg],
                             reads=[bS32[hg]], writes=[bS16[hg][1 - par]])
            P.op(DVE, "tensor_copy", sb32, pbank[ob], reads=[bps[ob]], writes=[bsb32])
            P.op(ACT, "activation", E2[0], sb32, AF.Square, accum_out=smallC[:, 0:1],
                 reads=[bsb32], writes=[bE[0], bsmC])
            rstd_from(smallC[:, 0:1], bsmC, RMS_EPS, 1.0 / 512.0, smallC[:, 1:2], bsmC)
            P.op(DVE, "scalar_tensor_tensor", sbbf, sb32, smallC[:, 1:2], gsb, ALU.mult, ALU.mult,
                 reads=[bsb32, bsmC, bgsb], writes=[bsbbf])
            transposes(sbbf, bsbbf, 4, 6 + (n % 2), QM[:, n, :, :], bQM[n], DVE)
        if debug:
            big_out(dbg["msb"], QM.rearrange("p n c q -> p (n c q)"), NSLOT * 512, 2048, bQM)
        P.barrier()

        CA.reset()
        wout = CA.bf16(8 * 1024).rearrange("p (c n) -> p c n", c=8)
        gbD = CA.f32(4096).rearrange("p (a d) -> p a d", a=4)
        xd2 = [CA.f32(1024) for _ in range(2)]
        hd2 = [CA.f32(1024) for _ in range(2)]
        r1 = [CA.f32(1024) for _ in range(2)]
        h1o = [CA.f32(1024) for _ in range(2)]
        bwo = P.buf("wout"); bgbD = P.buf("gbD"); bxd = P.bufs(2, "xd"); bhd = P.bufs(2, "hd")
        br1 = P.bufs(2, "r1"); bh1o = P.bufs(2, "h1o")
        bh1d = P.bufs(NSLOT, "h1d")
        set_stage([CA.f32(1024), CA.f32(1024)])
        wload(wout, wout_d, 8, 1024, "wout", bwo)
        for i, src_ in enumerate((lnin_g, lnin_b, ln1_g, ln1_b)):
            P.dma(SP, gbD[:, i, :], src_.partition_broadcast(128), "cD", writes=[bgbD])

        def loadD(n):
            P.dma(SP, xd2[n % 2], xown[n * 128:(n + 1) * 128, :], f"xd{n % 2}", writes=[bxd[n % 2]])

        loadD(0)
        for n in range(NSLOT if "D" in phases else 0):
            k2 = n % 2
            if n + 1 < NSLOT:
                loadD(n + 1)
            layer_norm(xd2[k2], bxd[k2], gbD[:, 0, :], gbD[:, 1, :], bgbD, hd2[k2], bhd[k2])
            b0 = 2 * k2
            pm = psum[:, b0:b0 + 2, :]
            for half in range(2):
                for c in range(8):
                    src_ = QM[:, n, c, :] if c < 4 else MSW[:, n, c - 4, :]
                    sbuf_ = bQM[n] if c < 4 else bMSW[n]
                    P.op(PE, "matmul", pm[:, half, :], src_, wout[:, c, half * 512:(half + 1) * 512],
                         start=(c == 0), stop=(c == 7), reads=[sbuf_, bwo], writes=[bps[b0 + half]])
            for half in range(2):
                P.op(DVE, "scalar_tensor_tensor", r1[k2][:, half * 512:(half + 1) * 512],
                     hd2[k2][:, half * 512:(half + 1) * 512], ALPHA, pm[:, half, :],
                     ALU.mult, ALU.add, reads=[bhd[k2], bps[b0 + half]], writes=[br1[k2]])
            layer_norm(r1[k2], br1[k2], gbD[:, 2, :], gbD[:, 3, :], bgbD, h1o[k2], bh1o[k2])
            P.dma(SP, h1_d[n * 128:(n + 1) * 128, :], h1o[k2], f"h1w{k2}", reads=[bh1o[k2]], writes=[bh1d[n]])
            if debug:
                P.dma(SP, dbg["h1"][n * 128:(n + 1) * 128, :], h1o[k2], "dbg", reads=[bh1o[k2]])
        P.barrier()

        CA.reset()
        wgu = CA.bf16(8 * 2 * DFF).rearrange("p (c n) -> p c n", c=8)
        wdn = CA.bf16(22 * 1024).rearrange("p (f n) -> p f n", f=22)
        CG.reset()
        h1t = [CG.f32(2048).rearrange("p (s d) -> p s d", s=2) for _ in range(2)]
        h1bf = CG.bf16(2048).rearrange("p (s d) -> p s d", s=2)
        h1T = CG.bf16(2048).rearrange("p (c q) -> p c q", c=8)
        actT = CG.bf16(22 * 256).rearrange("p (f q) -> p f q", f=22)
        sg2 = [CG.f32(256) for _ in range(2)]
        CP.reset()
        gbE = CP.f32(2048).rearrange("p (a d) -> p a d", a=2)
        r2 = [CP.f32(1024) for _ in range(2)]
        yo = [CP.f32(1024) for _ in range(2)]
        bwgu = P.buf("wgu"); bwdn = P.buf("wdn"); bh1t = P.bufs(2, "h1t"); bh1bf = P.buf(); bh1T = P.buf()
        bact = P.bufs(22, "act"); bsg = P.bufs(2, "sg"); bgbE = P.buf(); br2 = P.bufs(2, "r2"); byo = P.bufs(2, "yo")
        set_stage([CP.f32(1024), CP.f32(1024)])
        wload(wgu, wgu_d, 8, 2 * DFF, "wgu", bwgu)
        wload(wdn, wdn_d, 22, 1024, "wdn", bwdn)
        P.dma(SP, gbE[:, 0, :], ln2_g.partition_broadcast(128), "cE", writes=[bgbE])
        P.dma(SP, gbE[:, 1, :], ln2_b.partition_broadcast(128), "cE", writes=[bgbE])

        def loadE(T):
            P.dma(SP, h1t[T % 2], h1_d[T * 256:(T + 1) * 256, :].rearrange("(s p) d -> p s d", p=128), f"h1r{T % 2}",
                  reads=[bh1d[2 * T], bh1d[2 * T + 1]], writes=[bh1t[T % 2]])

        loadE(0)
        for T in range(NSLOT // 2 if "E" in phases else 0):
            t2 = T % 2
            if T + 1 < NSLOT // 2:
                loadE(T + 1)
            P.op(POOL, "tensor_copy", h1bf, h1t[t2], reads=[bh1t[t2]], writes=[bh1bf])
            ptv = pbf(0, 2).rearrange("p (c q) -> p c q", c=8)
            for s_ in range(2):
                for c in range(8):
                    P.op(PE, "transpose", ptv[:, c, s_ * 128:(s_ + 1) * 128], h1bf[:, s_, c * 128:(c + 1) * 128], ident,
                         reads=[bh1bf, bcmat], writes=[bps[0], bps[1]])
            P.op(DVE, "tensor_copy", h1T[:, 0:4, :], ptv[:, 0:4, :], reads=[bps[0]], writes=[bh1T])
            P.op(ACT, "copy", h1T[:, 4:8, :], ptv[:, 4:8, :], reads=[bps[1]], writes=[bh1T])
            for f in range(22):
                bk = 2 + (f % 2)
                for (off, col0) in ((0, f * 128), (256, DFF + f * 128)):
                    for c in range(8):
                        P.op(PE, "matmul", pbank[bk][:, off:off + 256], wgu[:, c, col0:col0 + 128], h1T[:, c, :],
                             start=(c == 0), stop=(c == 7), reads=[bwgu, bh1T], writes=[bps[bk]])
                k2 = f % 2
                P.op(ACT, "activation", sg2[k2], pbank[bk][:, 0:256], AF.Silu, reads=[bps[bk]], writes=[bsg[k2]])
                P.op(DVE, "tensor_tensor", actT[:, f, :], sg2[k2], pbank[bk][:, 256:512], ALU.mult,
                     reads=[bsg[k2], bps[bk]], writes=[bact[f]])
            for s_ in range(2):
                n = 2 * T + s_
                b0 = 4 + 2 * s_
                py = psum[:, b0:b0 + 2, :]
                for half in range(2):
                    for f in range(22):
                        P.op(PE, "matmul", py[:, half, :], actT[:, f, s_ * 128:(s_ + 1) * 128],
                             wdn[:, f, half * 512:(half + 1) * 512], start=(f == 0), stop=(f == 21),
                             reads=[bact[f], bwdn], writes=[bps[b0 + half]])
                for half in range(2):
                    P.op(DVE, "scalar_tensor_tensor", r2[s_][:, half * 512:(half + 1) * 512],
                         h1t[t2][:, s_, half * 512:(half + 1) * 512], ALPHA, py[:, half, :],
                         ALU.mult, ALU.add, reads=[bh1t[t2], bps[b0 + half]], writes=[br2[s_]])
                layer_norm(r2[s_], br2[s_], gbE[:, 0, :], gbE[:, 1, :], bgbE, yo[s_], byo[s_])
                P.dma(SP, y_d[n * 128:(n + 1) * 128, :], yo[s_], f"yw{s_}", reads=[byo[s_]])
        P.barrier()
        print("instr counts", {e: P.cnt[e] for e in P.cnt}, "waits", P.nwaits, "sems", len(P.semobj))
        P.emit()
    return nc


def _t5_bucket(dist):
    exact = 16
    d = np.maximum(dist, 0)
    d_f = np.maximum(d, 1).astype(np.float32)
    large = exact + (np.log(d_f / np.float32(exact)) / np.float32(np.log(128.0 / exact)) * np.float32(32 - exact)).astype(np.int32)
    large = np.minimum(large, 31)
    return np.where(d < exact, d, large)


def own_blocks(j):
    out = []
    for g in range(8):
        out += [8 * g + j, 8 * g + 7 - j]
    return out


_NC_CACHE = {}


def kernel(x, ln_in_g, ln_in_b, w_in, sb_norm_g, swa_norm_g, sinks, rel_bias, w_out, ln1_g, ln1_b,
           w_gate_up, w_down, ln2_g, ln2_b, _debug=False, _phases="ABCDE", _limit=None):
    f32 = np.float32
    x = np.asarray(x, f32)
    w_in0 = np.asarray(w_in, f32)[0]
    qsw = w_in0[:, 1536:2048].reshape(D, 8, 64)
    perm = [hh for a in range(4) for hh in (a, a + 4)]
    qsw = qsw[:, perm, :].reshape(D, 512)
    wA = np.ascontiguousarray(np.concatenate([w_in0[:, 0:512], qsw, w_in0[:, 2048:2304]], axis=1))
    wB = np.ascontiguousarray(w_in0[:, 512:1536])
    qi = np.arange(128)[:, None]
    cj = np.arange(256)[None, :]
    dist = qi + 128 - cj
    valid = (dist >= 0) & (dist < 128)
    bucket = _t5_bucket(dist)
    swbias = np.asarray(rel_bias, f32)[bucket]
    swbias = np.ascontiguousarray(swbias.transpose(0, 2, 1)).reshape(128, 8 * 256)
    mask_norm = np.where(valid, 0.0, NEG).astype(f32)
    mask_first = np.where(valid & (cj >= 128), 0.0, NEG).astype(f32)
    ident = np.eye(128, dtype=f32)
    jj = np.arange(128)[:, None]
    ss = np.arange(128)[None, :]
    trineg = np.where(jj >= ss, -1.0, 0.0).astype(f32)
    onesneg = -np.ones((128, 128), f32)
    cmat = np.concatenate([ident, trineg, onesneg], axis=1).astype(ml_dtypes.bfloat16)
    tri_mask = np.where(jj >= ss, NEG, 0.0).astype(f32)
    full_mask = np.full((128, 128), NEG, f32)
    zero_mask = np.zeros((128, 128), f32)

    shared = {
        "wA": wA, "wB": wB, "wout": np.ascontiguousarray(np.asarray(w_out, f32)[0]),
        "wgu": np.ascontiguousarray(np.asarray(w_gate_up, f32)[0]),
        "wdn": np.ascontiguousarray(np.asarray(w_down, f32)[0]),
        "lnin_g": np.asarray(ln_in_g, f32).reshape(1, D), "lnin_b": np.asarray(ln_in_b, f32).reshape(1, D),
        "ln1_g": np.asarray(ln1_g, f32).reshape(1, D), "ln1_b": np.asarray(ln1_b, f32).reshape(1, D),
        "ln2_g": np.asarray(ln2_g, f32).reshape(1, D), "ln2_b": np.asarray(ln2_b, f32).reshape(1, D),
        "gsb": np.asarray(sb_norm_g, f32).reshape(1, 512), "gsw": np.asarray(swa_norm_g, f32).reshape(1, 512),
        "sinks": np.asarray(sinks, f32).reshape(1, 8),
        "swbias": swbias, "cmat": cmat,
    }
    in_maps = []
    for c in range(8):
        b, j = c // 4, c % 4
        blocks = own_blocks(j)
        xo = np.concatenate([x[b, k * 128:(k + 1) * 128] for k in blocks], axis=0)
        xp = np.concatenate([x[b, (k - 1) * 128:k * 128] if k > 0 else np.zeros((128, D), f32) for k in blocks], axis=0)
        swmask = np.concatenate([mask_first if blocks[0] == 0 else mask_norm, mask_norm], axis=1)
        tiles = []
        for s_ in range(2):
            for m in range(4):
                dm = j if s_ == 0 else 3 - j
                tl = zero_mask if m < dm else (tri_mask if m == dm else full_mask)
                tiles.append(np.tile(tl, (1, 4)))
        sbmask = np.concatenate(tiles, axis=1).astype(ml_dtypes.bfloat16)
        d = dict(shared)
        d.update({"xall": np.ascontiguousarray(x[b]), "xown": np.ascontiguousarray(xo), "xprev": np.ascontiguousarray(xp),
                  "swmask": np.ascontiguousarray(swmask), "sbmask": np.ascontiguousarray(sbmask)})
        in_maps.append(d)

    key = (bool(_debug), _phases, _limit)
    if key not in _NC_CACHE:
        _NC_CACHE[key] = build(debug=_debug, phases=_phases, limit=_limit)
    nc = _NC_CACHE[key]
    res = run_bass_kernel_spmd(nc, in_maps, core_ids=list(range(8)))
    out = np.zeros((2, S, D), f32)
    for c in range(8):
        b, j = c // 4, c % 4
        yc = res.results[c]["y"]
        for n, k in enumerate(own_blocks(j)):
            out[b, k * 128:(k + 1) * 128] = yc[n * 128:(n + 1) * 128]
    if _debug:
        return out, res.results
    return out
```
